# Optimizing a Trainium2 kernel written in Bass

```python
import jax, jax.numpy as jnp
from jax import lax
import numpy as np

D_MODEL = 2048
BATCH = 16
SEQ = 256
DEPTH = 1
DEC_BATCH = 4
DEC_SEQ = 2048
PAST_LEN = 512

GRID_W = 64
N_HEADS_A = 16
HEAD_DIM_A = 64
WIDTH_A = N_HEADS_A * HEAD_DIM_A
NA_ROWS = 8
NA_COLS = 16
N_GROUPS_B = 8
GROUP_DIM_B = 128
WIDTH_B = N_GROUPS_B * GROUP_DIM_B
CHUNK = 128
MIX_WIDTH = WIDTH_A + WIDTH_B
IN_WIDTH = 3 * WIDTH_A + 2 * WIDTH_B
D_FF = 5632
N_MOD = 9
EPS = 1e-6
Q_BLOCK = 128
NEG_INF = -1e30

kernel_name = "hybrid_natten_gmlp_macaron_prefix_dit"


def rms_norm(x, g):
    xf = x.astype(jnp.float32)
    y = xf * lax.rsqrt(jnp.mean(xf * xf, axis=-1, keepdims=True) + EPS)
    return (y * g.astype(jnp.float32)).astype(x.dtype)


def adaln(cvec, w_ada, b_ada):
    m = jax.nn.silu(cvec) @ w_ada + b_ada
    return jnp.split(m, N_MOD, axis=-1)


def swiglu_sublayer(x, g, w_gate, w_up, w_down, shift, scale, gate):
    h = rms_norm(x, g) * (1 + scale) + shift
    f = (jax.nn.silu(h @ w_gate) * (h @ w_up)) @ w_down
    return x + 0.5 * gate * f


def mixer_inputs(x, g, w_in, shift, scale):
    B, L, _ = x.shape
    h = rms_norm(x, g) * (1 + scale) + shift
    p = h @ w_in
    q, k, v, u, vg = jnp.split(p, [WIDTH_A, 2 * WIDTH_A, 3 * WIDTH_A, 3 * WIDTH_A + WIDTH_B], axis=-1)
    shp = (B, L, N_HEADS_A, HEAD_DIM_A)
    return q.reshape(shp), k.reshape(shp), v.reshape(shp), u, vg


def spatial_gating(u, vg, norm_g, w_s, b_s):
    B, L, _ = u.shape
    u = jax.nn.gelu(u)
    vg = jax.nn.gelu(vg).reshape(B, L, N_GROUPS_B, GROUP_DIM_B)
    vg = rms_norm(vg, norm_g.reshape(N_GROUPS_B, GROUP_DIM_B))
    vc = vg.reshape(B, L // CHUNK, CHUNK, N_GROUPS_B, GROUP_DIM_B)
    s = jnp.einsum('gij,bnjgc->bnigc', w_s, vc) + b_s.T[None, None, :, :, None]
    return u * s.reshape(B, L, WIDTH_B)


def mixer_output(x, a, gm, na_g, nb_g, w_out, gate):
    y = jnp.concatenate([rms_norm(a, na_g), rms_norm(gm, nb_g)], axis=-1) @ w_out
    return x + gate * y


def context_attention(q, k, v):
    B, S, H, dh = q.shape
    qb = (q * dh ** -0.5).reshape(B, S // Q_BLOCK, Q_BLOCK, H, dh).transpose(1, 0, 2, 3, 4)

    def block(qi):
        s = jnp.einsum('bqhd,bkhd->bhqk', qi, k).astype(jnp.float32)
        p = jax.nn.softmax(s, axis=-1).astype(v.dtype)
        return jnp.einsum('bhqk,bkhd->bqhd', p, v)

    o = lax.map(block, qb)
    return o.transpose(1, 0, 2, 3, 4).reshape(B, S, H * dh)


def neighbourhood_attention(q, k, v, ctx_k, ctx_v, rpb):
    B, N, H, dh = q.shape
    rows = N // GRID_W
    kr = min(NA_ROWS, rows)
    r = jnp.arange(rows)
    row_start = jnp.clip(r - kr // 2, 0, rows - kr)
    row_idx = row_start[:, None] + jnp.arange(kr)[None, :]
    col = jnp.arange(GRID_W)
    col_start = jnp.clip(col - NA_COLS // 2, 0, GRID_W - NA_COLS)
    col_mask = (col[None, :] >= col_start[:, None]) & (col[None, :] < col_start[:, None] + NA_COLS)
    dr = row_idx - r[:, None]
    dc_idx = jnp.clip(col[None, :] - col[:, None] + NA_COLS - 1, 0, 2 * NA_COLS - 2)
    bias = rpb[:, dr + NA_ROWS - 1][..., dc_idx]
    bias = bias.transpose(0, 1, 3, 2, 4).astype(jnp.float32)
    bias = jnp.where(col_mask[None, None, :, None, :], bias, NEG_INF)

    qs = (q * dh ** -0.5).reshape(B, rows, GRID_W, H, dh)
    kg = k.reshape(B, rows, GRID_W, H, dh)[:, row_idx]
    vg = v.reshape(B, rows, GRID_W, H, dh)[:, row_idx]
    s_loc = jnp.einsum('brqhd,brkwhd->bhrqkw', qs, kg).astype(jnp.float32) + bias[None]
    s_ctx = jnp.einsum('brqhd,bphd->bhrqp', qs, ctx_k).astype(jnp.float32)
    n_loc = kr * GRID_W
    s = jnp.concatenate([s_loc.reshape(B, H, rows, GRID_W, n_loc), s_ctx], axis=-1)
    p = jax.nn.softmax(s, axis=-1).astype(v.dtype)
    p_loc = p[..., :n_loc].reshape(B, H, rows, GRID_W, kr, GRID_W)
    p_ctx = p[..., n_loc:]
    o = (jnp.einsum('bhrqkw,brkwhd->brqhd', p_loc, vg)
         + jnp.einsum('bhrqp,bphd->brqhd', p_ctx, ctx_v))
    return o.reshape(B, N, H * dh)


def setup_inputs(seed: int = 0) -> dict:
    key = jax.random.key(seed)
    ks = jax.random.split(key, 26)
    f32 = jnp.float32
    D, L = D_MODEL, DEPTH

    def nrm(k, shape, s):
        return jax.random.normal(k, shape, f32) * s

    def gain(k, shape):
        return 1.0 + 0.01 * jax.random.normal(k, shape, f32)

    return {
        "x_prompt": nrm(ks[0], (BATCH, SEQ, D), 1.0),
        "x_sample": nrm(ks[1], (DEC_BATCH, DEC_SEQ, D), 1.0),
        "cache_k": nrm(ks[2], (DEC_BATCH, L, PAST_LEN, N_HEADS_A, HEAD_DIM_A), 1.0),
        "cache_v": nrm(ks[3], (DEC_BATCH, L, PAST_LEN, N_HEADS_A, HEAD_DIM_A), 1.0),
        "c": nrm(ks[4], (DEC_BATCH, D), 1.0),
        "c_ctx": nrm(ks[5], (D,), 1.0),
        "w_ada": nrm(ks[6], (L, D, N_MOD * D), 0.1 * D ** -0.5),
        "b_ada": nrm(ks[7], (L, N_MOD * D), 0.01),
        "ffn1_norm": gain(ks[8], (L, D)),
        "ffn1_w_gate": nrm(ks[9], (L, D, D_FF), D ** -0.5),
        "ffn1_w_up": nrm(ks[10], (L, D, D_FF), D ** -0.5),
        "ffn1_w_down": nrm(ks[11], (L, D_FF, D), D_FF ** -0.5),
        "mix_norm": gain(ks[12], (L, D)),
        "w_in": nrm(ks[13], (L, D, IN_WIDTH), D ** -0.5),
        "rpb": nrm(ks[14], (L, N_HEADS_A, 2 * NA_ROWS - 1, 2 * NA_COLS - 1), 0.1),
        "gmlp_norm": gain(ks[15], (L, WIDTH_B)),
        "w_s": nrm(ks[16], (L, N_GROUPS_B, CHUNK, CHUNK), CHUNK ** -0.5),
        "b_s": gain(ks[17], (L, N_GROUPS_B, CHUNK)),
        "out_norm_a": gain(ks[18], (L, WIDTH_A)),
        "out_norm_b": gain(ks[19], (L, WIDTH_B)),
        "w_out": nrm(ks[20], (L, MIX_WIDTH, D), MIX_WIDTH ** -0.5),
        "ffn2_norm": gain(ks[21], (L, D)),
        "ffn2_w_gate": nrm(ks[22], (L, D, D_FF), D ** -0.5),
        "ffn2_w_up": nrm(ks[23], (L, D, D_FF), D ** -0.5),
        "ffn2_w_down": nrm(ks[24], (L, D_FF, D), D_FF ** -0.5),
        "final_norm": gain(ks[25], (D,)),
    }


def reference(x_prompt, x_sample, cache_k, cache_v, c, c_ctx, w_ada, b_ada,
              ffn1_norm, ffn1_w_gate, ffn1_w_up, ffn1_w_down,
              mix_norm, w_in, rpb, gmlp_norm, w_s, b_s, out_norm_a, out_norm_b, w_out,
              ffn2_norm, ffn2_w_gate, ffn2_w_up, ffn2_w_down, final_norm):
    x_ctx = x_prompt
    x_lat = x_sample
    new_k = []
    new_v = []
    for l in range(DEPTH):
        m_ctx = adaln(c_ctx, w_ada[l], b_ada[l])
        m_lat = [m[:, None, :] for m in adaln(c, w_ada[l], b_ada[l])]

        x_ctx = swiglu_sublayer(x_ctx, ffn1_norm[l], ffn1_w_gate[l], ffn1_w_up[l], ffn1_w_down[l],
                                m_ctx[0], m_ctx[1], m_ctx[2])
        q, k, v, u, vg = mixer_inputs(x_ctx, mix_norm[l], w_in[l], m_ctx[3], m_ctx[4])
        a = context_attention(q, k, v)
        gm = spatial_gating(u, vg, gmlp_norm[l], w_s[l], b_s[l])
        x_ctx = mixer_output(x_ctx, a, gm, out_norm_a[l], out_norm_b[l], w_out[l], m_ctx[5])
        x_ctx = swiglu_sublayer(x_ctx, ffn2_norm[l], ffn2_w_gate[l], ffn2_w_up[l], ffn2_w_down[l],
                                m_ctx[6], m_ctx[7], m_ctx[8])
        new_k.append(k)
        new_v.append(v)

        x_lat = swiglu_sublayer(x_lat, ffn1_norm[l], ffn1_w_gate[l], ffn1_w_up[l], ffn1_w_down[l],
                                m_lat[0], m_lat[1], m_lat[2])
        q, k, v, u, vg = mixer_inputs(x_lat, mix_norm[l], w_in[l], m_lat[3], m_lat[4])
        a = neighbourhood_attention(q, k, v, cache_k[:, l], cache_v[:, l], rpb[l])
        gm = spatial_gating(u, vg, gmlp_norm[l], w_s[l], b_s[l])
        x_lat = mixer_output(x_lat, a, gm, out_norm_a[l], out_norm_b[l], w_out[l], m_lat[5])
        x_lat = swiglu_sublayer(x_lat, ffn2_norm[l], ffn2_w_gate[l], ffn2_w_up[l], ffn2_w_down[l],
                                m_lat[6], m_lat[7], m_lat[8])

    y_prompt = rms_norm(x_ctx, final_norm)
    y_sample = rms_norm(x_lat, final_norm)
    state_k = jnp.stack(new_k, axis=1)
    state_v = jnp.stack(new_v, axis=1)
    return (y_prompt, y_sample, state_k, state_v)
```

```python
import contextlib
import numpy as np
import concourse.bass as bass
import concourse.mybir as mybir
from concourse.bass_utils import run_bass_kernel_spmd

F32 = mybir.dt.float32
BF16 = mybir.dt.bfloat16
AF = mybir.ActivationFunctionType
ALU = mybir.AluOpType

D = 2048
DFF = 5632
NFG = 22
NEG = -30000.0
EPS = 1e-6
N_CORES = 8
ARENA_WORDS = 52480


class _Node:
    __slots__ = ("w", "r", "ch")

    def __init__(self):
        self.w = None
        self.r = {}
        self.ch = {}


class Sched:
    ENG = ("pe", "act", "dve", "pool", "sp")

    def __init__(self):
        self.streams = {e: [] for e in self.ENG}
        self.count = {}
        self.known = {e: {} for e in self.ENG}
        self.flag = set()
        self.clock = {}
        self.root = _Node()
        self.dma_keys = []
        self.last_real = {}

    def _walk(self, region, create):
        node = self.root
        path = [node]
        for k in region:
            nxt = node.ch.get(k)
            if nxt is None:
                if not create:
                    return path, None
                nxt = _Node()
                node.ch[k] = nxt
            node = nxt
            path.append(node)
        return path, node

    def _subtree(self, node, out, with_reads):
        stack = [node]
        while stack:
            n = stack.pop()
            if n.w is not None:
                out.add(n.w)
            if with_reads:
                for ve, p in n.r.items():
                    out.add((ve, p))
            stack.extend(n.ch.values())

    def _deps(self, reads, writes):
        deps = set()
        for reg in reads:
            path, node = self._walk(reg, False)
            for n in path:
                if n.w is not None:
                    deps.add(n.w)
            if node is not None:
                self._subtree(node, deps, False)
        for reg in writes:
            path, node = self._walk(reg, False)
            for n in path:
                if n.w is not None:
                    deps.add(n.w)
                for ve, p in n.r.items():
                    deps.add((ve, p))
            if node is not None:
                self._subtree(node, deps, True)
        return deps

    def add(self, eng, fn, reads=(), writes=(), dma_key=None, extra_deps=()):
        veng = eng if dma_key is None else "dma:" + dma_key
        if dma_key is not None and veng not in self.count:
            self.dma_keys.append(veng)
        ps_r = [r for r in reads if r[0] == "ps"]
        if ps_r:
            reads = [r for r in reads if r[0] != "ps"]
            writes = list(writes) + ps_r
        deps = self._deps(reads, writes)
        deps.update(extra_deps)
        pos = self.count.get(veng, 0) + 1
        self.count[veng] = pos
        K = self.known[eng]
        waits = []
        for (dv, dp) in sorted(deps, key=lambda d: -d[1]):
            if dv == "pe" and eng == "pe" and dma_key is None:
                continue
            if K.get(dv, 0) >= dp:
                continue
            waits.append((dv, dp))
            self.flag.add((dv, dp))
            ck = self.clock[(dv, dp)]
            for k2, v2 in ck.items():
                if K.get(k2, 0) < v2:
                    K[k2] = v2
        ck = dict(K)
        ck[veng] = pos
        self.clock[(veng, pos)] = ck
        for reg in reads:
            _, node = self._walk(reg, True)
            if node.r.get(veng, 0) < pos:
                node.r[veng] = pos
        for reg in writes:
            _, node = self._walk(reg, True)
            node.ch = {}
            node.r = {}
            node.w = (veng, pos)
        self.streams[eng].append((fn, waits, veng, pos))
        if fn is not None:
            self.last_real[veng] = pos
        return (veng, pos)

    def barrier(self, final=False):
        snap = [(ve, p) for ve, p in self.last_real.items() if p > 0]
        for e in self.ENG:
            self.add(e, None, extra_deps=[d for d in snap if not (d[0] == e and e == "pe")])
        self.root = _Node()

    def emit(self, nc, block_fn_map):
        vengs = [e for e in self.ENG if self.count.get(e, 0) > 0] + self.dma_keys
        cum = {}
        for e in self.ENG:
            c = 0
            arr = [0] * (self.count.get(e, 0) + 1)
            for p in range(1, self.count.get(e, 0) + 1):
                if (e, p) in self.flag:
                    c += 1
                arr[p] = c
            cum[e] = arr
        with contextlib.ExitStack() as es:
            sems = {}
            for i, ve in enumerate(vengs):
                sems[ve] = es.enter_context(nc.semaphore("s%d" % i))
            with nc.Block() as block:
                def run(engname, handle):
                    for (fn, waits, veng, pos) in self.streams[engname]:
                        for (dv, dp) in waits:
                            if dv.startswith("dma:"):
                                handle.wait_ge(sems[dv], 16 * dp)
                            else:
                                handle.wait_ge(sems[dv], cum[dv][dp])
                        if fn is None:
                            continue
                        inst = fn(handle)
                        if veng.startswith("dma:"):
                            inst.then_inc(sems[veng], 16)
                        elif (veng, pos) in self.flag:
                            inst.then_inc(sems[veng], 1)

                @block.tensor
                def _(h):
                    run("pe", h)

                @block.scalar
                def _(h):
                    run("act", h)

                @block.vector
                def _(h):
                    run("dve", h)

                @block.gpsimd
                def _(h):
                    run("pool", h)

                @block.sync
                def _(h):
                    run("sp", h)


class Builder:
    def __init__(self, debug=None):
        self.debug = debug
        nc = bass.Bass("TRN2", target_bir_lowering=False)
        self.nc = nc
        self.S = Sched()
        dt = nc.dram_tensor

        def din(name, shape):
            return dt(name, list(shape), F32, kind="ExternalInput").ap()

        def dout(name, shape):
            return dt(name, list(shape), F32, kind="ExternalOutput").ap()

        self.xp = din("xp", [512, D])
        self.xs = din("xs", [1280, D])
        self.ck = din("ck", [512, 1024])
        self.cv = din("cv", [512, 1024])
        self.cvec = din("cvec", [128, 32])
        self.w_ada = din("w_ada", [72, 128, 16, 256])
        self.b_ada = din("b_ada_fm", [128, 144])
        self.gains = din("gains", [128, 64])
        self.gains_mix = din("gains_mix", [128, 16])
        self.gnorm_bc = din("gnorm_bc", [128, 1024])
        self.bs_bc = din("bs_bc", [128, 1024])
        self.w_sT = din("w_sT", [128, 1024])
        self.bs_bc_s = din("bs_bc_s", [128, 1024])
        self.w_sT_s = din("w_sT_s", [128, 1024])
        self.w1g = din("w1g", [NFG, 128, 16, 256])
        self.w1u = din("w1u", [NFG, 128, 16, 256])
        self.w1d = din("w1d", [NFG, 128, 2, D])
        self.w2g = din("w2g", [NFG, 128, 16, 256])
        self.w2u = din("w2u", [NFG, 128, 16, 256])
        self.w2d = din("w2d", [NFG, 128, 2, D])
        self.w_in = din("w_in", [20, 128, 16, 256])
        self.w_out = din("w_out", [8, 128, 16, 256])
        self.biasT = din("biasT", [16, 128, 1472])
        self.ind = din("ind", [28, 1280])
        self.rnz = din("rnz", [28, 1024])
        self.identd = din("ident", [128, 128])
        self.yp = dout("yp", [512, D])
        self.ys = dout("ys", [1024, D])
        self.sk = dout("sk", [512, 1024])
        self.sv = dout("sv", [512, 1024])
        self.xscr = dt("xscr", [128, 16 * 1024], F32, kind="Internal").ap()
        if debug is not None:
            self.dbg = dout("dbg", [128, debug["n"]])

    def a_f32(self, n):
        n8 = (n + 7) // 8 * 8
        off = self.ptr
        self.ptr += n8
        assert self.ptr <= ARENA_WORDS, ("arena overflow", self.ptr)
        return self.arena[:, off:off + n]

    def a_bf(self, n):
        w = (n + 1) // 2
        w8 = (w + 7) // 8 * 8
        off = self.ptr
        self.ptr += w8
        assert self.ptr <= ARENA_WORDS, ("arena overflow", self.ptr)
        return self.abf[:, 2 * off:2 * off + n]

    def bank(self, b, n=512, dtype=F32):
        return self.ps[:, b * 512:b * 512 + n]

    def mm(self, out, lhsT, rhs, start, stop, reads, writes):
        self.S.add("pe", lambda h: h.matmul(out, lhsT=lhsT, rhs=rhs, start=start, stop=stop),
                   reads=reads, writes=writes)

    def tr(self, out, in_, ident, reads, writes):
        self.S.add("pe", lambda h: h.transpose(out, in_, ident), reads=reads, writes=writes)

    def act(self, out, in_, func, reads, writes, scale=None, bias=None):
        kw = {}
        if scale is not None:
            kw["scale"] = scale
        if bias is not None:
            kw["bias"] = bias
        self.S.add("act", lambda h: h.activation(out, in_, func, **kw), reads=reads, writes=writes)

    def tt(self, out, in0, in1, op, reads, writes, eng="dve"):
        self.S.add(eng, lambda h: h.tensor_tensor(out, in0, in1, op), reads=reads, writes=writes)

    def ts(self, out, in0, s1, s2, op0, op1, reads, writes, eng="dve"):
        if op1 is None:
            self.S.add(eng, lambda h: h.tensor_scalar(out, in0, s1, None, op0), reads=reads, writes=writes)
        else:
            self.S.add(eng, lambda h: h.tensor_scalar(out, in0, s1, s2, op0, op1), reads=reads, writes=writes)

    def stt(self, out, in0, scalar, in1, op0, op1, reads, writes):
        self.S.add("dve", lambda h: h.scalar_tensor_tensor(out, in0, scalar, in1, op0, op1),
                   reads=reads, writes=writes)

    def cp(self, eng, out, in_, reads, writes):
        if eng == "act":
            self.S.add("act", lambda h: h.copy(out, in_), reads=reads, writes=writes)
        else:
            self.S.add(eng, lambda h: h.tensor_copy(out, in_), reads=reads, writes=writes)

    def recip(self, out, in_, reads, writes):
        self.S.add("dve", lambda h: h.reciprocal(out, in_), reads=reads, writes=writes)

    def dma(self, q, out, in_, key, reads=(), writes=()):
        self.S.add(q, lambda h: h.dma_start(out=out, in_=in_), reads=reads, writes=writes, dma_key=key)

    def build(self):
        nc = self.nc
        with contextlib.ExitStack() as es:
            self.arena = es.enter_context(nc.sbuf_tensor("arena", [128, ARENA_WORDS], F32))
            self.ps = es.enter_context(nc.psum_tensor("ps", [128, 4096], F32))
            self.abf = self.arena.bitcast(BF16)
            self.ptr = 0
            self.evq = 0
            self.consts()
            base = self.ptr
            self.kT_halo = self.a_bf(8 * 256).rearrange("p (c t) -> p c t", c=8)
            self.v_halo = self.a_bf(2 * 1024).rearrange("p (s c) -> p s c", s=2)
            gbase = self.ptr
            if self.run_group("PH", gbase):
                self.ptr = gbase
                self.run_group("S", gbase)
            self.S.barrier(final=True)
            self.S.emit(nc, None)
        return nc

    def evac_engine(self):
        self.evq += 1
        return "act" if (self.evq & 1) else "dve"

    def consts(self):
        S = self.S
        self.ident = self.a_f32(128)
        self.identb = self.a_bf(128)
        self.onesb = self.a_bf(128)
        self.modT = self.a_f32(288).rearrange("p (s m) -> p s m", s=2)
        self.coef = self.a_f32(160).rearrange("p (s k c) -> p s k c", s=2, k=5)
        self.gains_t = self.a_f32(64).rearrange("p (k c) -> p k c", k=4)
        self.gmix_t = self.a_f32(16)
        self.bada_t = self.a_f32(144)
        self.cvec_t = self.a_f32(32)
        self.scT = self.a_bf(32)
        self.gnorm_t = self.a_f32(1024)
        self.bs_t = self.a_f32(1024)
        self.wsT = self.a_bf(1024).rearrange("p (g i) -> p g i", g=8)
        self.bs_s_t = self.a_f32(1024)
        self.wsT_s = self.a_bf(1024).rearrange("p (g i) -> p g i", g=8)
        self.eps_t = self.a_f32(1)
        self.dma("sp", self.ident, self.identd, "c0", writes=[("ident",)])
        self.dma("sp", self.gains_t, self.gains.rearrange("p (k c) -> p k c", k=4), "c1", writes=[("gains",)])
        self.dma("sp", self.gmix_t, self.gains_mix, "c2", writes=[("gmix",)])
        self.dma("sp", self.bada_t, self.b_ada, "c3", writes=[("bada",)])
        self.dma("sp", self.cvec_t, self.cvec, "c4", writes=[("cvec",)])
        self.dma("sp", self.gnorm_t, self.gnorm_bc, "c5", writes=[("gnorm",)])
        self.dma("sp", self.bs_t, self.bs_bc, "c6", writes=[("bs",)])
        self.dma("pool", self.wsT, self.w_sT.rearrange("p (g i) -> p g i", g=8), "c7", writes=[("wsT",)])
        self.dma("sp", self.bs_s_t, self.bs_bc_s, "c10", writes=[("bs_s",)])
        self.dma("pool", self.wsT_s, self.w_sT_s.rearrange("p (g i) -> p g i", g=8), "c11", writes=[("wsT_s",)])
        S.add("dve", lambda h: h.memset(self.eps_t, EPS), writes=[("eps",)])
        self.cp("dve", self.identb, self.ident, [("ident",)], [("identb",)])
        S.add("dve", lambda h: h.memset(self.onesb, 1.0), writes=[("onesb",)])
        self.act(self.scT, self.cvec_t, AF.Silu, [("cvec",)], [("scT",)])

    def adaln_prefetch(self):
        save = self.ptr
        self.ptr = ARENA_WORDS - 3 * 2048 - 8
        self.ada_slabs = [self.a_bf(4096).rearrange("p (c f) -> p c f", c=16) for _ in range(3)]
        self.ptr = save
        self.ada_next = 0
        self.ada_loaded = 0
        self.ada_limit = 56
        self.ada_psb = self.bank(7, 288)
        for _ in range(3):
            self._ada_load()

    def adaln_begin(self):
        self._ada_evac(0, 32)
        for s in range(2):
            m = self.modT[:, s, :].rearrange("p (m c) -> p m c", m=9)
            self.stt(self.coef[:, s, 0, :], m[:, 1, :], 1.0, self.gains_t[:, 0, :], ALU.add, ALU.mult,
                     [("modT",), ("gains",)], [("coef", s, 0)])

    def adaln_gate1(self):
        for _ in range(8):
            self.adaln_slabs(1)
            yield
        self._ada_evac(32, 48)
        for s in range(2):
            m = self.modT[:, s, :].rearrange("p (m c) -> p m c", m=9)
            self.ts(self.coef[:, s, 1, :], m[:, 2, :], 0.5, None, ALU.mult, None, [("modT",)], [("coef", s, 1)])
        yield

    def _ada_load(self):
        i = self.ada_loaded
        if i >= self.ada_limit:
            return
        self.ada_loaded += 1
        self.dma("pool", self.ada_slabs[i % 3], self.w_ada[i], "ada%d" % (i % 3),
                 writes=[("aslab", i % 3)])

    def adaln_slabs(self, n):
        for _ in range(n):
            i = self.ada_next
            if i >= self.ada_limit:
                return
            self.ada_next += 1
            for cc in range(2):
                j = 2 * i + cc
                for dc in range(16):
                    self.mm(self.ada_psb[:, 2 * j:2 * j + 2], self.ada_slabs[i % 3][:, dc, cc * 128:(cc + 1) * 128],
                            self.scT[:, 2 * dc:2 * dc + 2], dc == 0, dc == 15,
                            [("aslab", i % 3), ("scT",)], [("ps", 7)])
            self._ada_load()

    def _ada_evac(self, j0, j1):
        psv = self.ada_psb.rearrange("p (j s) -> p j s", s=2)
        for s in range(2):
            self.tt(self.modT[:, s, j0:j1], psv[:, j0:j1, s], self.bada_t[:, j0:j1], ALU.add,
                    [("ps", 7), ("bada",)], [("modT", j0)])

    def adaln_hook(self, fg):
        self.adaln_slabs(2 if (fg % 2 == 0 and fg < 20) else 1)

    def adaln_finish(self):
        self.adaln_slabs(72)
        self._ada_evac(48, 2 * self.ada_limit)
        for s in range(2):
            m = self.modT[:, s, :].rearrange("p (m c) -> p m c", m=9)
            self.stt(self.coef[:, s, 2, :], m[:, 4, :], 1.0, self.gains_t[:, 1, :], ALU.add, ALU.mult,
                     [("modT",), ("gains",)], [("coef", s, 2)])
        self.S.barrier()

    def adaln_resume(self):
        self.ada_limit = 72
        for _ in range(3):
            self._ada_load()

    def adaln_finish2(self):
        self.adaln_slabs(72)
        self._ada_evac(112, 144)
        for s in range(2):
            m = self.modT[:, s, :].rearrange("p (m c) -> p m c", m=9)
            self.stt(self.coef[:, s, 3, :], m[:, 7, :], 1.0, self.gains_t[:, 2, :], ALU.add, ALU.mult,
                     [("modT",), ("gains",)], [("coef", s, 3)])
            self.ts(self.coef[:, s, 4, :], m[:, 8, :], 0.5, None, ALU.mult, None, [("modT",)], [("coef", s, 4)])
        self.S.barrier()

    def mod(self, s, mi):
        return self.modT[:, s, mi * 16:(mi + 1) * 16]

    def normmod(self, src, dst, tiles, nch, Dn, scale_of, bias_of, srcname, dstname, tmps, psbanks):
        self.norm_stats(src, tiles, nch, Dn, srcname, tmps, psbanks)
        self.norm_apply(src, dst, tiles, nch, scale_of, bias_of, srcname, dstname, tmps)

    def norm_stats(self, src, tiles, nch, Dn, srcname, tmps, psbanks):
        sq, std, rstd, tmp = tmps
        assert len(tiles) <= 2
        for ti, (c0, n, tix) in enumerate(tiles):
            pb = psbanks[ti % len(psbanks)]
            pbank = self.bank(pb, n)
            ngr = nch // 4
            for g in range(ngr):
                sqb = sq[g % 2]
                self.act(sqb[:, :, 0:n], src[:, 4 * g:4 * g + 4, c0:c0 + n], AF.Square,
                         [(srcname, tix, c) for c in range(4 * g, 4 * g + 4)], [("sq", g % 2)])
                for r in range(4):
                    c = 4 * g + r
                    self.mm(pbank, self.onesb, sqb[:, r, 0:n], c == 0, c == nch - 1,
                            [("sq", g % 2), ("onesb",)], [("ps", pb)])
            self.act(std[ti][:, 0:n], pbank, AF.Sqrt, [("ps", pb)], [("std", ti)], scale=1.0 / Dn, bias=self.eps_t)
            self.recip(rstd[ti][:, 0:n], std[ti][:, 0:n], [("std", ti)], [("rstd", ti)])

    def norm_apply(self, src, dst, tiles, nch, scale_of, bias_of, srcname, dstname, tmps, extra_reads=()):
        sq, std, rstd, tmp = tmps
        for ti, (c0, n, tix) in enumerate(tiles):
            for c in range(nch):
                tb = tmp[c % 2]
                self.tt(tb[:, 0:n], src[:, c, c0:c0 + n], rstd[ti][:, 0:n], ALU.mult,
                        [(srcname, tix, c), ("rstd", ti)], [("ntmp", c % 2)])
                sc = scale_of(c, tix)
                bi = bias_of(c, tix) if bias_of is not None else None
                self.act(dst[:, c, c0:c0 + n], tb[:, 0:n], AF.Identity, [("ntmp", c % 2)] + list(extra_reads),
                         [(dstname, tix, c)], scale=sc, bias=bi)

    def ffn(self, G, Wg, Wu, Wd, tiles, set_of, gk, slabs, tmps, hook=None, dn_banks=(4, 5, 6, 7), first_gen=None,
            early_loads=None):
        wg, wu, wd = slabs
        sg, actb = tmps
        xT, hT = G["xT"], G["hT"]

        def load(fg, alias=()):
            b = fg % 2
            self.dma("pool", wg[b], Wg[fg], "wg%d" % b, writes=[("wg", b)] + [(a, b) for a in alias])
            self.dma("pool", wu[b], Wu[fg], "wu%d" % b, writes=[("wu", b)])
            self.dma("pool", wd[b], Wd[fg], "wd%d" % b, writes=[("wd", b)])

        if early_loads == "issue":
            load(0, alias=("stage",))
            load(1, alias=("stage",))
            return

        units = [(fg, t) for fg in range(NFG) for t in range(len(tiles))]
        state = {"it": 0, "dn": 0}

        def gateup(ui):
            fg, t = units[ui]
            c0, n, tix = tiles[t]
            b = fg % 2
            for fc in range(2):
                it = state["it"]
                state["it"] += 1
                pg, pu = it % 2, 2 + it % 2
                for dc in range(16):
                    self.mm(self.bank(pg, n), wg[b][:, dc, fc * 128:(fc + 1) * 128], hT[:, dc, c0:c0 + n],
                            dc == 0, dc == 15, [("wg", b), ("hT", tix, dc)], [("ps", pg)])
                    if dc % 4 == 3:
                        if dc == 15:
                            self.act(sg[it % 2][:, 0:n], self.bank(pg, n), AF.Silu, [("ps", pg)], [("sg", it % 2)])
                        yield
                for dc in range(16):
                    self.mm(self.bank(pu, n), wu[b][:, dc, fc * 128:(fc + 1) * 128], hT[:, dc, c0:c0 + n],
                            dc == 0, dc == 15, [("wu", b), ("hT", tix, dc)], [("ps", pu)])
                    if dc % 4 == 3:
                        if dc == 15:
                            self.tt(actb[ui % 2][:, fc, 0:n], sg[it % 2][:, 0:n], self.bank(pu, n), ALU.mult,
                                    [("sg", it % 2), ("ps", pu)], [("actb", ui % 2, fc)])
                        yield

        def down(ui):
            fg, t = units[ui]
            c0, n, tix = tiles[t]
            b = fg % 2
            s = set_of(tix)
            for dc in range(16):
                pd = dn_banks[state["dn"] % len(dn_banks)]
                state["dn"] += 1
                for fc in range(2):
                    self.mm(self.bank(pd, n), wd[b][:, fc, dc * 128:(dc + 1) * 128], actb[ui % 2][:, fc, 0:n],
                            fc == 0, fc == 1, [("wd", b), ("actb", ui % 2, fc)], [("ps", pd)])
                self.stt(xT[:, dc, c0:c0 + n], self.bank(pd, n), self.coef[:, s, gk, dc:dc + 1],
                         xT[:, dc, c0:c0 + n], ALU.mult, ALU.add,
                         [("ps", pd), ("xT", tix, dc), ("coef", s, gk)], [("xT", tix, dc)])
                if dc == 15 and t == len(tiles) - 1:
                    if fg + 2 < NFG:
                        load(fg + 2)
                    if hook is not None:
                        hook(fg)
                yield

        if early_loads != "done":
            load(0)
            load(1)
        for ui in range(len(units)):
            g = gateup(ui)
            d = down(ui - 1) if ui > 0 else None
            for ch in range(16):
                next(g)
                if ui == 0 and first_gen is not None and ch % 2 == 1:
                    next(first_gen, None)
                if d is not None and ch >= 4:
                    next(d)
                    if ch in (7, 10, 13, 15):
                        next(d)
            for _ in g:
                pass
            if ui == 0 and first_gen is not None:
                for _ in first_gen:
                    pass
            if d is not None:
                for _ in d:
                    pass
        for _ in down(len(units) - 1):
            pass

    def gelu(self, psrc, out, n, reads, writes, t1, t2, k):
        self.act(t1, psrc, AF.Square, reads, [("gt1", k)])
        self.ts(t1, t1, 0.044715, 1.0, ALU.mult, ALU.add, [("gt1", k)], [("gt1", k)])
        self.tt(t2, t1, psrc, ALU.mult, [("gt1", k)] + list(reads), [("gt2", k)])
        self.act(t2, t2, AF.Sigmoid, [("gt2", k)], [("gt2", k)], scale=1.5957691216057308)
        self.tt(out, t2, psrc, ALU.mult, [("gt2", k)] + list(reads), writes)

    def run_group(self, name, gbase):
        S = self.S
        isS = (name == "S")
        if not isS:
            n_all, n_own = 768, 512
            tiles_all = [(0, 512, 0), (512, 256, 1)]
            tiles_own = [(0, 512, 0)]
            set_of = lambda tix: 0 if tix == 0 else 1
        else:
            n_all, n_own = 1024, 1024
            tiles_all = [(0, 512, 0), (512, 512, 1)]
            tiles_own = tiles_all
            set_of = lambda tix: 1
        nsub_all, nsub_own = n_all // 128, n_own // 128
        G = {}
        self.ptr = gbase
        r0 = self.ptr
        hT = self.a_bf(16 * n_all).rearrange("p (c t) -> p c t", c=16)
        r1 = self.ptr
        xT = self.a_f32(16 * n_all).rearrange("p (c t) -> p c t", c=16)
        G["xT"], G["hT"] = xT, hT
        pbase = self.ptr

        def xrows(ts):
            if isS:
                return self.xs[ts * 128:(ts + 1) * 128, :]
            if ts < 4:
                return self.xp[ts * 128:(ts + 1) * 128, :]
            return self.xs[1024 + (ts - 4) * 128:1024 + (ts - 3) * 128, :]

        def norm_tmps():
            sq = [self.a_bf(4 * 512).rearrange("p (r t) -> p r t", r=4) for _ in range(2)]
            std = [self.a_f32(512) for _ in range(2)]
            rstd = [self.a_f32(512) for _ in range(2)]
            tmp = [self.a_f32(512) for _ in range(2)]
            return sq, std, rstd, tmp

        def ffn_bufs():
            wg = [self.a_bf(4096).rearrange("p (c f) -> p c f", c=16) for _ in range(2)]
            wu = [self.a_bf(4096).rearrange("p (c f) -> p c f", c=16) for _ in range(2)]
            wd = [self.a_bf(4096).rearrange("p (fc d) -> p fc d", fc=2) for _ in range(2)]
            sg = [self.a_f32(512) for _ in range(2)]
            actb = [self.a_bf(1024).rearrange("p (fc t) -> p fc t", fc=2) for _ in range(2)]
            return (wg, wu, wd), (sg, actb)

        if not isS:
            self.adaln_prefetch()
        nt = norm_tmps()
        fbase = self.ptr
        stage = [self.a_f32(2048) for _ in range(2)]
        for ts in range(nsub_all):
            sb = ts % 2
            tix = ts // 4
            self.dma("sp", stage[sb], xrows(ts), "xst%d" % sb, writes=[("stage", sb)])
            for q in range(4):
                pb = 4 * sb + q
                for r in range(4):
                    dc = 4 * q + r
                    self.tr(self.bank(pb)[:, r * 128:(r + 1) * 128], stage[sb][:, dc * 128:(dc + 1) * 128], self.ident,
                            [("stage", sb), ("ident",)], [("ps", pb)])
                self.cp(self.evac_engine(), xT[:, 4 * q:4 * q + 4, ts * 128:(ts + 1) * 128],
                        self.bank(pb).rearrange("p (r t) -> p r t", r=4),
                        [("ps", pb)], [("xT", tix, 4 * q + r) for r in range(4)])
        self.ptr = fbase
        slabs, ftmps = ffn_bufs()

        if not isS:
            self.adaln_slabs(16)
        self.norm_stats(xT, tiles_all, 16, D, "xT", nt, [4, 5])
        if not isS:
            self.ffn(G, self.w1g, self.w1u, self.w1d, tiles_all, set_of, 1, slabs, ftmps, early_loads="issue")
            self.adaln_begin()
            xr = [("coef",), ("modT",)]
        else:
            S.barrier()
            xr = []
        self.norm_apply(xT, hT, tiles_all, 16,
                        lambda c, tix: self.coef[:, set_of(tix), 0, c:c + 1],
                        lambda c, tix: self.mod(set_of(tix), 0)[:, c:c + 1],
                        "xT", "hT", nt, extra_reads=xr)
        if self.debug and self.debug.get("stop") == name + ":norm1":
            return self.dump(hT)
        if not isS:
            self.ffn(G, self.w1g, self.w1u, self.w1d, tiles_all, set_of, 1, slabs, ftmps,
                     hook=self.adaln_hook, dn_banks=(4, 5, 6), first_gen=self.adaln_gate1(), early_loads="done")
            self.adaln_finish()
        else:
            self.ffn(G, self.w1g, self.w1u, self.w1d, tiles_all, set_of, 1, slabs, ftmps)
        if self.debug and self.debug.get("stop") == name + ":ffn1":
            return self.dump(xT)

        if not isS:
            save = self.ptr
            self.ptr = pbase + 5 * (8 * n_own) // 2
            ws_pre = [self.a_bf(4096).rearrange("p (c f) -> p c f", c=16) for _ in range(4)]
            self.ptr = save
            for si in range(4):
                self.dma("pool", ws_pre[si], self.w_in[si], "ws%d" % si, writes=[("ws", si)])
        self.normmod(xT, hT, tiles_all, 16, D,
                     lambda c, tix: self.coef[:, set_of(tix), 2, c:c + 1],
                     lambda c, tix: self.mod(set_of(tix), 3)[:, c:c + 1],
                     "xT", "hT", nt, [4, 5])
        if isS:
            self.dma("sp", self.xscr, self.arena_flat(xT), "spill", reads=[("xT",)], writes=[("xscr",)])
        S.barrier()
        self.ptr = r1 if isS else pbase
        kT = self.a_bf(8 * n_own).rearrange("p (c t) -> p c t", c=8)
        v = self.a_bf(nsub_own * 1024).rearrange("p (s c) -> p s c", s=nsub_own)
        qT = self.a_bf(8 * n_own).rearrange("p (c t) -> p c t", c=8)
        mbase = self.ptr
        guT = self.a_bf(8 * n_own).rearrange("p (c t) -> p c t", c=8)
        vn = self.a_bf(nsub_own * 1024).rearrange("p (s c) -> p s c", s=nsub_own)
        gbase2 = self.ptr
        ws = [self.a_bf(4096).rearrange("p (c f) -> p c f", c=16) for _ in range(4)]
        gt1 = [self.a_f32(512) for _ in range(2)]
        gt2 = [self.a_f32(512) for _ in range(2)]
        gv = [self.a_f32(1024) for _ in range(2)]
        gsq = self.a_f32(1024)
        gss = self.a_f32(8)
        gstd = self.a_f32(8)
        grs = self.a_f32(8)
        ost = [self.a_f32(256) for _ in range(2)]

        def wload(si):
            self.dma("pool", ws[si % 4], self.w_in[si], "ws%d" % (si % 4), writes=[("ws", si % 4)])

        if isS:
            for si in range(4):
                wload(si)
        else:
            assert all(ws[k].offset == ws_pre[k].offset for k in range(4))
        fmb = [0]
        tmb = [0]
        gk = [0]
        for si in range(16):
            typ, w = si // 4, si % 4
            sl = ws[si % 4]
            rs = [("ws", si % 4)]
            if typ in (0, 1, 3):
                ttiles = tiles_all if typ == 1 else tiles_own
                for (c0, n, tix) in ttiles:
                    for cc in range(2):
                        pb = fmb[0] % 4
                        fmb[0] += 1
                        for dc in range(16):
                            self.mm(self.bank(pb, n), sl[:, dc, cc * 128:(cc + 1) * 128], hT[:, dc, c0:c0 + n],
                                    dc == 0, dc == 15, rs + [("hT", tix, dc)], [("ps", pb)])
                        ch = 2 * w + cc
                        if typ == 0:
                            self.cp(self.evac_engine(), qT[:, ch, c0:c0 + n], self.bank(pb, n), [("ps", pb)], [("qT", ch, tix)])
                        elif typ == 1:
                            if (not isS) and tix == 1:
                                dstk, wr = self.kT_halo[:, ch, 0:n], [("kTh", ch)]
                            else:
                                dstk, wr = kT[:, ch, c0:c0 + n], [("kT", ch, tix)]
                            self.cp(self.evac_engine(), dstk, self.bank(pb, n), [("ps", pb)], wr)
                        else:
                            k = gk[0] % 2
                            gk[0] += 1
                            self.gelu(self.bank(pb, n), guT[:, ch, c0:c0 + n], n, [("ps", pb)], [("guT", ch, tix)],
                                      gt1[k][:, 0:n], gt2[k][:, 0:n], k)
            if typ in (1, 2):
                if typ == 1 and isS:
                    subs = []
                elif typ == 1:
                    subs = list(range(4))
                else:
                    subs = list(range(nsub_all))
                for ts in subs:
                    pb = 4 + tmb[0] % 4
                    tmb[0] += 1
                    tix = ts // 4
                    for dc in range(16):
                        self.mm(self.bank(pb, 256), hT[:, dc, ts * 128:(ts + 1) * 128], sl[:, dc, :],
                                dc == 0, dc == 15, rs + [("hT", tix, dc)], [("ps", pb)])
                    if typ == 2:
                        if (not isS) and ts >= 4:
                            dstv, wr = self.v_halo[:, ts - 4, w * 256:(w + 1) * 256], [("vh", ts - 4, w)]
                        else:
                            dstv, wr = v[:, ts, w * 256:(w + 1) * 256], [("v", ts, w)]
                        self.cp(self.evac_engine(), dstv, self.bank(pb, 256), [("ps", pb)], wr)
                    if (not isS) and ts < 4:
                        ob = tmb[0] % 2
                        self.cp(self.evac_engine(), ost[ob], self.bank(pb, 256), [("ps", pb)], [("ost", ob)])
                        dst = (self.sk if typ == 1 else self.sv)[ts * 128:(ts + 1) * 128, w * 256:(w + 1) * 256]
                        self.dma("sp", dst, ost[ob], "ost%d" % ob, reads=[("ost", ob)])
            if si + 4 < 20:
                wload(si + 4)
        gnv = self.gnorm_t
        for ts in range(nsub_own):
            tix = ts // 4
            k = ts % 2
            for half in range(2):
                pb = 4 + tmb[0] % 4
                tmb[0] += 1
                for hh in range(2):
                    si = 16 + 2 * half + hh
                    for dc in range(16):
                        self.mm(self.bank(pb)[:, hh * 256:(hh + 1) * 256], hT[:, dc, ts * 128:(ts + 1) * 128],
                                ws[si % 4][:, dc, :], dc == 0, dc == 15,
                                [("ws", si % 4), ("hT", tix, dc)], [("ps", pb)])
                self.gelu(self.bank(pb), gv[k][:, half * 512:(half + 1) * 512], 512, [("ps", pb)], [("gv", k, half)],
                          gt1[half][:, 0:512], gt2[half][:, 0:512], half)
            self.act(gsq, gv[k], AF.Square, [("gv", k)], [("gsq",)])
            S.add("dve", lambda h, o=gss, i=gsq: h.tensor_reduce(o, i.rearrange("p (g c) -> p g c", g=8),
                                                                  mybir.AxisListType.X, ALU.add),
                  reads=[("gsq",)], writes=[("gss",)])
            self.act(gstd, gss, AF.Sqrt, [("gss",)], [("gstd",)], scale=1.0 / 128, bias=self.eps_t)
            self.recip(grs, gstd, [("gstd",)], [("grs",)])
            g3 = gv[k].rearrange("p (g c) -> p g c", g=8)
            self.tt(g3, g3, grs.unsqueeze(2).broadcast_to([128, 8, 128]), ALU.mult,
                    [("gv", k), ("grs",)], [("gv", k)])
            self.tt(vn[:, ts, :], gv[k], gnv, ALU.mult, [("gv", k), ("gnorm",)], [("vn", ts)])
        if self.debug and self.debug.get("stop") == name + ":mixin":
            which = self.debug.get("which", "qT")
            return self.dump({"qT": qT, "kT": kT, "v": v, "guT": guT, "vn": vn}[which])

        S.barrier()
        self.ptr = r0
        mix = self.a_bf(16 * n_own).rearrange("p (c t) -> p c t", c=16)
        assert self.ptr <= r1
        self.ptr = gbase2
        gtmp = [self.a_f32(512) for _ in range(2)]
        if isS:
            self.attn_sample_prep()
        else:
            self.adaln_resume()
        bs_use = self.bs_s_t if isS else self.bs_t
        ws_use = self.wsT_s if isS else self.wsT
        wsname = "wsT_s" if isS else "wsT"
        bsname = "bs_s" if isS else "bs"
        gi = 0
        for ts in range(nsub_own):
            tix = ts // 4
            for half in range(2):
                pb = gi % 4
                k = gi % 2
                gi += 1
                for q in range(4):
                    g = half * 4 + q
                    self.mm(self.bank(pb)[:, q * 128:(q + 1) * 128], vn[:, ts, g * 128:(g + 1) * 128], ws_use[:, g, :],
                            True, True, [("vn", ts), (wsname,)], [("ps", pb)])
                self.tt(gtmp[k], self.bank(pb), bs_use[:, half * 512:(half + 1) * 512], ALU.add,
                        [("ps", pb), (bsname,)], [("gtmp", k)])
                self.tt(mix[:, 8 + 4 * half:12 + 4 * half, ts * 128:(ts + 1) * 128],
                        gtmp[k].rearrange("p (g i) -> p g i", g=4),
                        guT[:, 4 * half:4 * half + 4, ts * 128:(ts + 1) * 128], ALU.mult,
                        [("gtmp", k)] + [("guT", 4 * half + q, tix) for q in range(4)],
                        [("mix", 8 + 4 * half + q, tix) for q in range(4)])
        S.barrier()
        self.ptr = mbase

        if not isS:
            self.attn_prompt(qT, kT, v, mix)
        else:
            self.attn_sample(qT, kT, v, mix)
        if self.debug and self.debug.get("stop") == name + ":attn":
            return self.dump(mix)
        S.barrier()
        self.ptr = pbase

        if isS:
            self.dma("sp", self.arena_flat(xT), self.xscr, "reload", reads=[("xscr",)], writes=[("xT",)])
        nt = norm_tmps()
        wo = [self.a_bf(4096).rearrange("p (c f) -> p c f", c=16) for _ in range(4)]

        def oload(i):
            self.dma("pool", wo[i % 4], self.w_out[i], "wo%d" % (i % 4), writes=[("wo", i % 4)])

        for i in range(4):
            oload(i)
        sset = 1 if isS else 0
        for half in range(2):
            self.normmod(mix[:, 8 * half:8 * half + 8, :], mix[:, 8 * half:8 * half + 8, :], tiles_own, 8, 1024,
                         lambda c, tix, half=half: self.gmix_t[:, 8 * half + c:8 * half + c + 1], None,
                         "mix%d" % half, "mix%d" % half, nt, [5, 6])
        ob = 0
        for i in range(8):
            for (c0, n, tix) in tiles_own:
                for cc in range(2):
                    pb = ob % 4
                    ob += 1
                    dcx = 2 * i + cc
                    for mc in range(16):
                        self.mm(self.bank(pb, n), wo[i % 4][:, mc, cc * 128:(cc + 1) * 128], mix[:, mc, c0:c0 + n],
                                mc == 0, mc == 15, [("wo", i % 4), ("mix%d" % (mc // 8), tix, mc % 8)], [("ps", pb)])
                    self.stt(xT[:, dcx, c0:c0 + n], self.bank(pb, n), self.mod(sset, 5)[:, dcx:dcx + 1],
                             xT[:, dcx, c0:c0 + n], ALU.mult, ALU.add,
                             [("ps", pb), ("xT", tix, dcx), ("modT",)], [("xT", tix, dcx)])
            if i + 4 < 8:
                oload(i + 4)
            if not isS:
                self.adaln_slabs(1)
        if self.debug and self.debug.get("stop") == name + ":mixout":
            return self.dump(xT)
        S.barrier()
        self.ptr = pbase

        nt = norm_tmps()
        slabs, ftmps = ffn_bufs()
        if not isS:
            self.adaln_finish2()
        self.normmod(xT, hT, tiles_own, 16, D,
                     lambda c, tix: self.coef[:, sset, 3, c:c + 1],
                     lambda c, tix: self.mod(sset, 6)[:, c:c + 1],
                     "xT", "hT", nt, [4, 5])
        self.ffn(G, self.w2g, self.w2u, self.w2d, tiles_own, lambda tix: sset, 4, slabs, ftmps)
        wg_ = slabs[0]
        ostage = [wg_[sb].rearrange("p c f -> p (c f)").bitcast(F32) for sb in range(2)]
        ydst = self.ys if isS else self.yp
        self.norm_stats(xT, tiles_own, 16, D, "xT", nt, [4, 5])
        for ti, (c0, n, tix) in enumerate(tiles_own):
            self.norm_apply(xT, xT, [tiles_own[ti]], 16,
                            lambda c, tix_: self.gains_t[:, 3, c:c + 1], None, "xT", "xT",
                            (nt[0], nt[1][ti:ti + 1], nt[2][ti:ti + 1], nt[3]))
            for ts in range(c0 // 128, (c0 + n) // 128):
                sb = ts % 2
                for q in range(4):
                    pb = 4 * sb + q
                    for r in range(4):
                        dc = 4 * q + r
                        self.tr(self.bank(pb)[:, r * 128:(r + 1) * 128], xT[:, dc, ts * 128:(ts + 1) * 128], self.ident,
                                [("xT", tix, dc), ("ident",)], [("ps", pb)])
                    self.cp(self.evac_engine(), ostage[sb][:, q * 512:(q + 1) * 512], self.bank(pb),
                            [("ps", pb)], [("ostage", sb, q), ("wg", sb)])
                self.dma("sp", ydst[ts * 128:(ts + 1) * 128, :], ostage[sb], "yst%d" % sb,
                         reads=[("ostage", sb), ("wg", sb)])
        S.barrier()
        return True

    def arena_flat(self, view3):
        return view3.rearrange("p c t -> p (c t)")

    def attn_prompt(self, qT, kT, v, mix):
        E = [self.a_bf(512) for _ in range(4)]
        rden = [self.a_f32(256) for _ in range(2)]
        jobs = [(s, w, hh) for s in range(2) for w in range(8) for hh in range(2)]

        def stage_a(it):
            s, w, hh = jobs[it]
            off = 64 * hh
            pb = it % 4
            for kc in range(2):
                tsb = 2 * s + kc
                self.mm(self.bank(pb)[:, kc * 256:(kc + 1) * 256], kT[off:off + 64, w, tsb * 128:(tsb + 1) * 128],
                        qT[off:off + 64, w, s * 256:(s + 1) * 256], True, True,
                        [("kT", w, 0), ("qT", w, 0)], [("ps", pb)])
            self.act(E[pb], self.bank(pb), AF.Exp, [("ps", pb)], [("E", pb)], scale=0.125)

        def stage_b(it):
            s, w, hh = jobs[it]
            po = 4 + (it % 2)
            eb = E[it % 4]
            for kc in range(2):
                tsb = 2 * s + kc
                self.mm(self.bank(po)[:, 0:256], v[:, tsb, w * 128:(w + 1) * 128], eb[:, kc * 256:(kc + 1) * 256],
                        kc == 0, kc == 1, [("v", tsb), ("E", it % 4)], [("ps", po)])
            for kc in range(2):
                self.mm(self.bank(po)[:, 256:512], self.onesb, eb[:, kc * 256:(kc + 1) * 256], kc == 0, kc == 1,
                        [("onesb",), ("E", it % 4)], [("ps", po)])
            if it % 3 == 2 and it >= 5:
                self.adaln_slabs(1)

        def stage_c(it):
            s, w, hh = jobs[it]
            off = 64 * hh
            po = 4 + (it % 2)
            rd = rden[it % 2]
            self.act(rd[off:off + 64], self.bank(po)[off:off + 64, 256:512], AF.Ln, [("ps", po)], [("rden", it % 2)])
            self.act(rd[off:off + 64], rd[off:off + 64], AF.Exp, [("rden", it % 2)], [("rden", it % 2)], scale=-1.0)
            self.tt(mix[off:off + 64, w, s * 256:(s + 1) * 256], self.bank(po)[off:off + 64, 0:256],
                    rd[off:off + 64], ALU.mult, [("ps", po), ("rden", it % 2)], [("mix", w, 0, hh, s)])

        n = len(jobs)
        for it in range(n + 3):
            if it < n:
                stage_a(it)
            if 0 <= it - 2 < n:
                stage_b(it - 2)
            if 0 <= it - 3 < n:
                stage_c(it - 3)

    def attn_sample_prep(self):
        S = self.S
        ckT = self.a_bf(8 * 512).rearrange("p (c t) -> p c t", c=8)
        cvb = self.a_bf(4 * 1024).rearrange("p (s c) -> p s c", s=4)
        cst = [self.a_f32(1024) for _ in range(2)]
        kTz = [self.a_bf(1280) for _ in range(2)]
        qTz = [self.a_bf(1024) for _ in range(2)]
        ckTz = [self.a_bf(512) for _ in range(2)]
        for hh in range(2):
            for buf, nm in ((kTz, "kTz"), (qTz, "qTz"), (ckTz, "ckTz")):
                S.add("dve", lambda h, b=buf[hh]: h.memset(b, 0.0), writes=[(nm, hh)])
            o2 = 64 * (1 - hh)
            self.dma("pool", kTz[hh][o2:o2 + 28], self.ind, "ind%d" % hh, writes=[("kTz", hh)])
            self.dma("pool", qTz[hh][o2:o2 + 28], self.rnz, "rnz%d" % hh, writes=[("qTz", hh)])
        self.dma("pool", cvb, self.cv.rearrange("(s p) c -> p s c", p=128), "cvb", writes=[("cvb",)])
        for ts in range(4):
            sb = ts % 2
            self.dma("sp", cst[sb], self.ck[ts * 128:(ts + 1) * 128, :], "cst%d" % sb, writes=[("cst", sb)])
            for q in range(2):
                pb = 4 + 2 * sb + q
                for r in range(4):
                    c = 4 * q + r
                    self.tr(self.bank(pb)[:, r * 128:(r + 1) * 128], cst[sb][:, c * 128:(c + 1) * 128], self.ident,
                            [("cst", sb), ("ident",)], [("ps", pb)])
                self.cp(self.evac_engine(), ckT[:, 4 * q:4 * q + 4, ts * 128:(ts + 1) * 128],
                        self.bank(pb).rearrange("p (r t) -> p r t", r=4), [("ps", pb)],
                        [("ckT", 4 * q + r, ts) for r in range(4)])
        self.attn_bufs = (ckT, cvb, kTz, qTz, ckTz)

    def attn_sample(self, qT, kT, v, mix):
        S = self.S
        ckT, cvb, kTz, qTz, ckTz = self.attn_bufs
        ebst = self.a_f32(1472)
        eb = [self.a_bf(1472) for _ in range(2)]
        E = [self.a_bf(512) for _ in range(3)]
        PT = [self.a_bf(512) for _ in range(4)]
        rden = [self.a_f32(512) for _ in range(2)]
        chunks = {0: [0, 2, 4, 6, 8, 10], 1: [4, 6, 8, 10, 12, 14, 16, 18]}
        posof = {}
        p = 0
        for t in (0, 1):
            for l0 in chunks[t]:
                posof[(t, l0)] = p
                p += 1
        def head_copies(hd):
            w_, hh_ = hd // 2, hd % 2
            o_ = 64 * hh_
            self.cp("dve", kTz[hh_][o_:o_ + 64, 0:1024], kT[o_:o_ + 64, w_, :], [("kT", w_)], [("kTz", hh_)])
            self.cp("dve", kTz[hh_][o_:o_ + 64, 1024:1280], self.kT_halo[o_:o_ + 64, w_, :], [("kTh", w_)], [("kTz", hh_)])
            self.cp("dve", qTz[hh_][o_:o_ + 64, :], qT[o_:o_ + 64, w_, :], [("qT", w_)], [("qTz", hh_)])
            self.cp("dve", ckTz[hh_][o_:o_ + 64, :], ckT[o_:o_ + 64, w_, :], [("ckT", w_)], [("ckTz", hh_)])

        def _valid(par, l, j):
            gj = j if par == 0 else 31 - j
            gl = l if par == 0 else 31 - l
            rs = min(max(gj - 4, 0), 24)
            return rs <= gl <= rs + 7

        jrange = {}
        for t_ in (0, 1):
            for l0_ in chunks[t_]:
                js = [jj for jj in range(8) if any(_valid(p_, l0_ + a_, 8 * t_ + jj) for p_ in (0, 1) for a_ in (0, 1))]
                jrange[(t_, l0_)] = (min(js), max(js) + 1)

        def head_eb(hd):
            self.dma("sp", ebst, self.biasT[hd], "ebst", writes=[("ebst",)])
            self.act(eb[hd % 2], ebst, AF.Exp, [("ebst",)], [("eb", hd % 2)])

        def qk(job, cidx, ci):
            w, hh, t, h, off, q0, po, pd, nch, it = job
            ebb = eb[h % 2]
            pb = ci % 4
            ptb = PT[ci % 4]
            if cidx >= 4:
                l0 = chunks[t][cidx - 4]
                ja, jb = jrange[(t, l0)]
                ca, cb = ja * 64, jb * 64
                if l0 < 16:
                    vsrc = v[:, l0 // 2, w * 128:(w + 1) * 128]
                    vreg = ("v", l0 // 2)
                else:
                    vsrc = self.v_halo[:, (l0 - 16) // 2, w * 128:(w + 1) * 128]
                    vreg = ("vh", (l0 - 16) // 2)
                self.mm(self.bank(pb)[:, ca:cb], kTz[hh][:, l0 * 64:l0 * 64 + 128], qTz[hh][:, q0 + ca:q0 + cb], True, True,
                        [("kTz", hh), ("qTz", hh)], [("ps", pb)])
                ebuf = E[ci % 3]
                self.act(ebuf[:, ca:cb], self.bank(pb)[:, ca:cb], AF.Exp, [("ps", pb)], [("E", ci % 3)], scale=0.125)
                ei0 = 8 * t - l0 + 11
                self.tt(ptb[:, ca:cb], ebuf[:, ca:cb], ebb[:, (ei0 + ja) * 64:(ei0 + jb) * 64], ALU.mult,
                        [("E", ci % 3), ("eb", h % 2)], [("PT", ci % 4)])
            else:
                cc = cidx
                ca, cb = 0, 512
                self.mm(self.bank(pb), ckTz[hh][:, cc * 128:(cc + 1) * 128], qTz[hh][:, q0:q0 + 512], True, True,
                        [("ckTz", hh), ("qTz", hh)], [("ps", pb)])
                self.act(ptb, self.bank(pb), AF.Exp, [("ps", pb)], [("PT", ci % 4)], scale=0.125)
                vsrc = cvb[:, cc, w * 128:(w + 1) * 128]
                vreg = ("cvb",)
            return (vsrc, vreg, ptb, ci % 4, ca, cb)

        def pv(job, cidx, st):
            w, hh, t, h, off, q0, po, pd, nch, it = job
            vsrc, vreg, ptb, pti, ca, cb = st
            self.mm(self.bank(po)[:, ca:cb], vsrc, ptb[:, ca:cb], cidx == 0, cidx == nch - 1,
                    [vreg, ("PT", pti)], [("ps", po)])
            self.mm(self.bank(pd)[:, ca:cb], self.onesb, ptb[:, ca:cb], cidx == 0, cidx == nch - 1,
                    [("onesb",), ("PT", pti)], [("ps", pd)])

        def finalize(job):
            w, hh, t, h, off, q0, po, pd, nch, it = job
            rd = rden[it % 2]
            self.act(rd[off:off + 64], self.bank(pd)[off:off + 64], AF.Ln, [("ps", pd)], [("rden", it % 2)])
            self.act(rd[off:off + 64], rd[off:off + 64], AF.Exp, [("rden", it % 2)], [("rden", it % 2)], scale=-1.0)
            self.tt(mix[off:off + 64, w, q0:q0 + 512], self.bank(po)[off:off + 64], rd[off:off + 64], ALU.mult,
                    [("ps", po), ("rden", it % 2)], [("mix", w, t, hh)])

        head_copies(0)
        head_eb(0)
        steps = []
        it = 0
        for h in range(16):
            w, hh = h // 2, h % 2
            for t in range(2):
                nch = len(chunks[t]) + 4
                job = (w, hh, t, h, 64 * hh, t * 512, 4 + (it % 2), 6 + (it % 2), nch, it)
                it += 1
                for cidx in range(nch):
                    steps.append((job, cidx))
        staged = {}

        def retire(k):
            jb, cb = steps[k]
            pv(jb, cb, staged.pop(k))
            if cb == jb[8] - 1:
                finalize(jb)

        for k, (job, cidx) in enumerate(steps):
            if cidx == 0 and job[2] == 1 and job[3] + 1 < 16:
                head_copies(job[3] + 1)
                head_eb(job[3] + 1)
            staged[k] = qk(job, cidx, k)
            if k >= 2:
                retire(k - 2)
        retire(len(steps) - 2)
        retire(len(steps) - 1)

    def dump(self, view):
        S = self.S
        S.barrier()
        n = self.debug["n"]
        shp = view.shape
        if len(shp) == 3:
            flat = view.rearrange("p c t -> p (c t)")
        else:
            flat = view
        if flat.dtype != F32:
            self.ptr = ARENA_WORDS - ((n + 7) // 8 * 8) - 8
            tmp = self.a_f32(n)
            self.cp("dve", tmp, flat[:, 0:n], [], [("dbgtmp",)])
            flat = tmp
        self.dma("sp", self.dbg, flat[:, 0:n], "dbg", reads=[("dbgtmp",)])
        return False


def _fm(vec, nch):
    return np.ascontiguousarray(np.asarray(vec, np.float32).reshape(nch, 128).T)


def _bias_table(rpb_l, par):
    kc = np.arange(64)[:, None]
    qc = np.arange(64)[None, :]
    dcidx = np.clip(kc - qc + 15, 0, 30)
    cs = np.clip(qc - 8, 0, 48)
    colvalid = (kc >= cs) & (kc < cs + 16)
    T = np.zeros((16, 64, 23, 64), np.float32)
    for ei in range(23):
        e = ei - 11
        if abs(e) <= 7:
            dr = -e if par == 0 else e
            vals = rpb_l[:, dr + 7][:, dcidx]
            T[:, :, ei, :] = np.where(colvalid[None], vals, np.float32(NEG))
    T2 = np.zeros_like(T)
    T2[:, :, 1:, :] = T[:, :, :-1, :]
    T = np.concatenate([T, T2], axis=1)
    return np.ascontiguousarray(T.reshape(16, 128, 23 * 64))


def _masks(par):
    chunks = {0: [0, 2, 4, 6, 8, 10], 1: [4, 6, 8, 10, 12, 14, 16, 18]}
    ind = np.zeros((28, 1280), np.float32)
    rnz = np.zeros((28, 1024), np.float32)
    pos = 0
    for t in (0, 1):
        for l0 in chunks[t]:
            for a in range(2):
                l = l0 + a
                ind[2 * pos + a, l * 64:(l + 1) * 64] = 1.0
                for jj in range(8):
                    j = 8 * t + jj
                    gj = j if par == 0 else 31 - j
                    gl = l if par == 0 else 31 - l
                    rs = min(max(gj - 4, 0), 24)
                    valid = rs <= gl <= rs + 7
                    rnz[2 * pos + a, j * 64:(j + 1) * 64] = 0.0 if valid else NEG
            pos += 1
    return ind, rnz


_NC_CACHE = {}


def _get_nc(debug=None):
    key = None if debug is None else tuple(sorted(debug.items()))
    if key not in _NC_CACHE:
        _NC_CACHE[key] = Builder(debug).build()
    return _NC_CACHE[key]


def make_in_maps(x_prompt, x_sample, cache_k, cache_v, c, c_ctx, w_ada, b_ada,
                 ffn1_norm, ffn1_w_gate, ffn1_w_up, ffn1_w_down,
                 mix_norm, w_in, rpb, gmlp_norm, w_s, b_s, out_norm_a, out_norm_b, w_out,
                 ffn2_norm, ffn2_w_gate, ffn2_w_up, ffn2_w_down, final_norm):
    f = lambda a: np.ascontiguousarray(np.asarray(a, np.float32))

    def kslab(w):
        w = np.asarray(w, np.float32)
        n = w.shape[1] // 256
        return np.ascontiguousarray(w.reshape(16, 128, n, 256).transpose(2, 1, 0, 3))

    def dslab(w):
        w = np.asarray(w, np.float32)
        return np.ascontiguousarray(w.reshape(NFG, 2, 128, D).transpose(0, 2, 1, 3))
    shared = {
        "w_ada": kslab(w_ada[0]),
        "b_ada_fm": _fm(b_ada[0], 144),
        "gains": np.ascontiguousarray(np.concatenate(
            [_fm(ffn1_norm[0], 16), _fm(mix_norm[0], 16), _fm(ffn2_norm[0], 16), _fm(final_norm, 16)], axis=1)),
        "gains_mix": np.ascontiguousarray(np.concatenate([_fm(out_norm_a[0], 8), _fm(out_norm_b[0], 8)], axis=1)),
        "gnorm_bc": np.ascontiguousarray(np.broadcast_to(np.asarray(gmlp_norm[0], np.float32)[None, :], (128, 1024))),
        "bs_bc": np.ascontiguousarray(np.broadcast_to(np.asarray(b_s[0], np.float32).reshape(1, 1024), (128, 1024))),
        "w_sT": np.ascontiguousarray(np.asarray(w_s[0], np.float32).transpose(2, 0, 1).reshape(128, 1024)),
        "w1g": kslab(ffn1_w_gate[0]), "w1u": kslab(ffn1_w_up[0]), "w1d": dslab(ffn1_w_down[0]),
        "w2g": kslab(ffn2_w_gate[0]), "w2u": kslab(ffn2_w_up[0]), "w2d": dslab(ffn2_w_down[0]),
        "w_in": kslab(w_in[0]), "w_out": kslab(w_out[0]),
        "ident": np.eye(128, dtype=np.float32),
    }
    rpb_l = np.asarray(rpb[0], np.float32)
    par_tabs = {}
    for par in (0, 1):
        ind, rnz = _masks(par)
        par_tabs[par] = (_bias_table(rpb_l, par), ind, rnz)
    in_maps = []
    for core in range(N_CORES):
        b, par = core // 2, core % 2
        xs_full = np.asarray(x_sample[b], np.float32).reshape(32, 64, D)
        if par == 1:
            xs_full = xs_full[::-1]
        xs = np.ascontiguousarray(xs_full[0:20].reshape(1280, D))
        cv2 = np.stack([_fm(c_ctx, 16), _fm(c[b], 16)], axis=2).reshape(128, 32)
        ws_l = np.asarray(w_s[0], np.float32)
        bs_l = np.asarray(b_s[0], np.float32)
        if par == 1:
            perm = np.concatenate([np.arange(64, 128), np.arange(0, 64)])
            ws_l = ws_l[:, perm][:, :, perm]
            bs_l = bs_l[:, perm]
        m = dict(shared)
        m.update({
            "w_sT_s": np.ascontiguousarray(ws_l.transpose(2, 0, 1).reshape(128, 1024)),
            "bs_bc_s": np.ascontiguousarray(np.broadcast_to(bs_l.reshape(1, 1024), (128, 1024))),
            "xp": np.ascontiguousarray(np.asarray(x_prompt[2 * core:2 * core + 2], np.float32).reshape(512, D)),
            "xs": xs,
            "ck": np.ascontiguousarray(np.asarray(cache_k[b, 0], np.float32).reshape(512, 1024)),
            "cv": np.ascontiguousarray(np.asarray(cache_v[b, 0], np.float32).reshape(512, 1024)),
            "cvec": np.ascontiguousarray(cv2),
            "biasT": par_tabs[par][0], "ind": par_tabs[par][1], "rnz": par_tabs[par][2],
        })
        in_maps.append(m)
    return in_maps


def kernel(**inputs):
    nc = _get_nc()
    in_maps = make_in_maps(**inputs)
    res = run_bass_kernel_spmd(nc, in_maps, core_ids=list(range(N_CORES)))
    y_prompt = np.zeros((16, 256, D), np.float32)
    y_sample = np.zeros((4, 2048, D), np.float32)
    state_k = np.zeros((16, 1, 256, 16, 64), np.float32)
    state_v = np.zeros((16, 1, 256, 16, 64), np.float32)
    for core in range(N_CORES):
        r = res.results[core]
        b, par = core // 2, core % 2
        y_prompt[2 * core:2 * core + 2] = r["yp"].reshape(2, 256, D)
        state_k[2 * core:2 * core + 2, 0] = r["sk"].reshape(2, 256, 16, 64)
        state_v[2 * core:2 * core + 2, 0] = r["sv"].reshape(2, 256, 16, 64)
        ys = r["ys"].reshape(16, 64, D)
        yv = y_sample[b].reshape(32, 64, D)
        if par == 0:
            yv[0:16] = ys
        else:
            yv[16:32] = ys[::-1]
    return (y_prompt, y_sample, state_k, state_v)
```

```python
import contextlib
import numpy as np
import concourse.bass as bass
import concourse.mybir as mybir
from concourse.bass_utils import run_bass_kernel_spmd

F32 = mybir.dt.float32
BF16 = mybir.dt.bfloat16
AF = mybir.ActivationFunctionType
ALU = mybir.AluOpType

D = 2048
DFF = 5632
NFG = 22
NEG = -30000.0
EPS = 1e-6
N_CORES = 8
ARENA_WORDS = 52480


class _Node:
    __slots__ = ("w", "r", "ch")

    def __init__(self):
        self.w = None
        self.r = {}
        self.ch = {}


class Sched:
    ENG = ("pe", "act", "dve", "pool", "sp")

    def __init__(self):
        self.streams = {e: [] for e in self.ENG}
        self.count = {}
        self.known = {e: {} for e in self.ENG}
        self.flag = set()
        self.clock = {}
        self.root = _Node()
        self.dma_keys = []
        self.last_real = {}

    def _walk(self, region, create):
        node = self.root
        path = [node]
        for k in region:
            nxt = node.ch.get(k)
            if nxt is None:
                if not create:
                    return path, None
                nxt = _Node()
                node.ch[k] = nxt
            node = nxt
            path.append(node)
        return path, node

    def _subtree(self, node, out, with_reads):
        stack = [node]
        while stack:
            n = stack.pop()
            if n.w is not None:
                out.add(n.w)
            if with_reads:
                for ve, p in n.r.items():
                    out.add((ve, p))
            stack.extend(n.ch.values())

    def _deps(self, reads, writes):
        deps = set()
        for reg in reads:
            path, node = self._walk(reg, False)
            for n in path:
                if n.w is not None:
                    deps.add(n.w)
            if node is not None:
                self._subtree(node, deps, False)
        for reg in writes:
            path, node = self._walk(reg, False)
            for n in path:
                if n.w is not None:
                    deps.add(n.w)
                for ve, p in n.r.items():
                    deps.add((ve, p))
            if node is not None:
                self._subtree(node, deps, True)
        return deps

    def add(self, eng, fn, reads=(), writes=(), dma_key=None, extra_deps=()):
        veng = eng if dma_key is None else "dma:" + dma_key
        if dma_key is not None and veng not in self.count:
            self.dma_keys.append(veng)
        ps_r = [r for r in reads if r[0] == "ps"]
        if ps_r:
            reads = [r for r in reads if r[0] != "ps"]
            writes = list(writes) + ps_r
        deps = self._deps(reads, writes)
        deps.update(extra_deps)
        pos = self.count.get(veng, 0) + 1
        self.count[veng] = pos
        K = self.known[eng]
        waits = []
        for (dv, dp) in sorted(deps, key=lambda d: -d[1]):
            if dv == "pe" and eng == "pe" and dma_key is None:
                continue
            if K.get(dv, 0) >= dp:
                continue
            waits.append((dv, dp))
            self.flag.add((dv, dp))
            ck = self.clock[(dv, dp)]
            for k2, v2 in ck.items():
                if K.get(k2, 0) < v2:
                    K[k2] = v2
        ck = dict(K)
        ck[veng] = pos
        self.clock[(veng, pos)] = ck
        for reg in reads:
            _, node = self._walk(reg, True)
            if node.r.get(veng, 0) < pos:
                node.r[veng] = pos
        for reg in writes:
            _, node = self._walk(reg, True)
            node.ch = {}
            node.r = {}
            node.w = (veng, pos)
        self.streams[eng].append((fn, waits, veng, pos))
        if fn is not None:
            self.last_real[veng] = pos
        return (veng, pos)

    def barrier(self, final=False):
        snap = [(ve, p) for ve, p in self.last_real.items() if p > 0]
        for e in self.ENG:
            self.add(e, None, extra_deps=[d for d in snap if not (d[0] == e and e == "pe")])
        self.root = _Node()

    def emit(self, nc, block_fn_map):
        vengs = [e for e in self.ENG if self.count.get(e, 0) > 0] + self.dma_keys
        cum = {}
        for e in self.ENG:
            c = 0
            arr = [0] * (self.count.get(e, 0) + 1)
            for p in range(1, self.count.get(e, 0) + 1):
                if (e, p) in self.flag:
                    c += 1
                arr[p] = c
            cum[e] = arr
        with contextlib.ExitStack() as es:
            sems = {}
            for i, ve in enumerate(vengs):
                sems[ve] = es.enter_context(nc.semaphore("s%d" % i))
            with nc.Block() as block:
                def run(engname, handle):
                    for (fn, waits, veng, pos) in self.streams[engname]:
                        for (dv, dp) in waits:
                            if dv.startswith("dma:"):
                                handle.wait_ge(sems[dv], 16 * dp)
                            else:
                                handle.wait_ge(sems[dv], cum[dv][dp])
                        if fn is None:
                            continue
                        inst = fn(handle)
                        if veng.startswith("dma:"):
                            inst.then_inc(sems[veng], 16)
                        elif (veng, pos) in self.flag:
                            inst.then_inc(sems[veng], 1)

                @block.tensor
                def _(h):
                    run("pe", h)

                @block.scalar
                def _(h):
                    run("act", h)

                @block.vector
                def _(h):
                    run("dve", h)

                @block.gpsimd
                def _(h):
                    run("pool", h)

                @block.sync
                def _(h):
                    run("sp", h)


class Builder:
    def __init__(self, debug=None):
        self.debug = debug
        nc = bass.Bass("TRN2", target_bir_lowering=False)
        self.nc = nc
        self.S = Sched()
        dt = nc.dram_tensor

        def din(name, shape):
            return dt(name, list(shape), F32, kind="ExternalInput").ap()

        def dout(name, shape):
            return dt(name, list(shape), F32, kind="ExternalOutput").ap()

        self.xp = din("xp", [512, D])
        self.xs = din("xs", [1280, D])
        self.ck = din("ck", [512, 1024])
        self.cv = din("cv", [512, 1024])
        self.cvec = din("cvec", [128, 32])
        self.w_ada = din("w_ada", [72, 128, 16, 256])
        self.b_ada = din("b_ada_fm", [128, 144])
        self.gains = din("gains", [128, 64])
        self.gains_mix = din("gains_mix", [128, 16])
        self.gnorm_bc = din("gnorm_bc", [128, 1024])
        self.bs_bc = din("bs_bc", [128, 1024])
        self.w_sT = din("w_sT", [128, 1024])
        self.bs_bc_s = din("bs_bc_s", [128, 1024])
        self.w_sT_s = din("w_sT_s", [128, 1024])
        self.w1g = din("w1g", [NFG, 128, 16, 256])
        self.w1u = din("w1u", [NFG, 128, 16, 256])
        self.w1d = din("w1d", [NFG, 128, 2, D])
        self.w2g = din("w2g", [NFG, 128, 16, 256])
        self.w2u = din("w2u", [NFG, 128, 16, 256])
        self.w2d = din("w2d", [NFG, 128, 2, D])
        self.w_in = din("w_in", [20, 128, 16, 256])
        self.w_out = din("w_out", [8, 128, 16, 256])
        self.biasT = din("biasT", [16, 128, 1472])
        self.ind = din("ind", [28, 1280])
        self.rnz = din("rnz", [28, 1024])
        self.identd = din("ident", [128, 128])
        self.yp = dout("yp", [512, D])
        self.ys = dout("ys", [1024, D])
        self.sk = dout("sk", [512, 1024])
        self.sv = dout("sv", [512, 1024])
        self.xscr = dt("xscr", [128, 16 * 1024], F32, kind="Internal").ap()
        if debug is not None:
            self.dbg = dout("dbg", [128, debug["n"]])

    def a_f32(self, n):
        n8 = (n + 7) // 8 * 8
        off = self.ptr
        self.ptr += n8
        assert self.ptr <= ARENA_WORDS, ("arena overflow", self.ptr)
        return self.arena[:, off:off + n]

    def a_bf(self, n):
        w = (n + 1) // 2
        w8 = (w + 7) // 8 * 8
        off = self.ptr
        self.ptr += w8
        assert self.ptr <= ARENA_WORDS, ("arena overflow", self.ptr)
        return self.abf[:, 2 * off:2 * off + n]

    def bank(self, b, n=512, dtype=F32):
        return self.ps[:, b * 512:b * 512 + n]

    def mm(self, out, lhsT, rhs, start, stop, reads, writes):
        self.S.add("pe", lambda h: h.matmul(out, lhsT=lhsT, rhs=rhs, start=start, stop=stop),
                   reads=reads, writes=writes)

    def tr(self, out, in_, ident, reads, writes):
        self.S.add("pe", lambda h: h.transpose(out, in_, ident), reads=reads, writes=writes)

    def act(self, out, in_, func, reads, writes, scale=None, bias=None):
        kw = {}
        if scale is not None:
            kw["scale"] = scale
        if bias is not None:
            kw["bias"] = bias
        self.S.add("act", lambda h: h.activation(out, in_, func, **kw), reads=reads, writes=writes)

    def tt(self, out, in0, in1, op, reads, writes, eng="dve"):
        self.S.add(eng, lambda h: h.tensor_tensor(out, in0, in1, op), reads=reads, writes=writes)

    def ts(self, out, in0, s1, s2, op0, op1, reads, writes, eng="dve"):
        if op1 is None:
            self.S.add(eng, lambda h: h.tensor_scalar(out, in0, s1, None, op0), reads=reads, writes=writes)
        else:
            self.S.add(eng, lambda h: h.tensor_scalar(out, in0, s1, s2, op0, op1), reads=reads, writes=writes)

    def stt(self, out, in0, scalar, in1, op0, op1, reads, writes):
        self.S.add("dve", lambda h: h.scalar_tensor_tensor(out, in0, scalar, in1, op0, op1),
                   reads=reads, writes=writes)

    def cp(self, eng, out, in_, reads, writes):
        if eng == "act":
            self.S.add("act", lambda h: h.copy(out, in_), reads=reads, writes=writes)
        else:
            self.S.add(eng, lambda h: h.tensor_copy(out, in_), reads=reads, writes=writes)

    def recip(self, out, in_, reads, writes):
        self.S.add("dve", lambda h: h.reciprocal(out, in_), reads=reads, writes=writes)

    def dma(self, q, out, in_, key, reads=(), writes=()):
        self.S.add(q, lambda h: h.dma_start(out=out, in_=in_), reads=reads, writes=writes, dma_key=key)

    def build(self):
        nc = self.nc
        with contextlib.ExitStack() as es:
            self.arena = es.enter_context(nc.sbuf_tensor("arena", [128, ARENA_WORDS], F32))
            self.ps = es.enter_context(nc.psum_tensor("ps", [128, 4096], F32))
            self.abf = self.arena.bitcast(BF16)
            self.ptr = 0
            self.evq = 0
            self.consts()
            base = self.ptr
            self.kT_halo = self.a_bf(8 * 256).rearrange("p (c t) -> p c t", c=8)
            self.v_halo = self.a_bf(2 * 1024).rearrange("p (s c) -> p s c", s=2)
            gbase = self.ptr
            if self.run_group("PH", gbase):
                self.ptr = gbase
                self.run_group("S", gbase)
            self.S.barrier(final=True)
            self.S.emit(nc, None)
        return nc

    def evac_engine(self):
        self.evq += 1
        return "act" if (self.evq & 1) else "dve"

    def consts(self):
        S = self.S
        self.ident = self.a_f32(128)
        self.identb = self.a_bf(128)
        self.onesb = self.a_bf(128)
        self.modT = self.a_f32(288).rearrange("p (s m) -> p s m", s=2)
        self.coef = self.a_f32(160).rearrange("p (s k c) -> p s k c", s=2, k=5)
        self.gains_t = self.a_f32(64).rearrange("p (k c) -> p k c", k=4)
        self.gmix_t = self.a_f32(16)
        self.bada_t = self.a_f32(144)
        self.cvec_t = self.a_f32(32)
        self.scT = self.a_bf(32)
        self.gnorm_t = self.a_f32(1024)
        self.bs_t = self.a_f32(1024)
        self.wsT = self.a_bf(1024).rearrange("p (g i) -> p g i", g=8)
        self.bs_s_t = self.a_f32(1024)
        self.wsT_s = self.a_bf(1024).rearrange("p (g i) -> p g i", g=8)
        self.eps_t = self.a_f32(1)
        self.dma("sp", self.ident, self.identd, "c0", writes=[("ident",)])
        self.dma("sp", self.gains_t, self.gains.rearrange("p (k c) -> p k c", k=4), "c1", writes=[("gains",)])
        self.dma("sp", self.gmix_t, self.gains_mix, "c2", writes=[("gmix",)])
        self.dma("sp", self.bada_t, self.b_ada, "c3", writes=[("bada",)])
        self.dma("sp", self.cvec_t, self.cvec, "c4", writes=[("cvec",)])
        self.dma("sp", self.gnorm_t, self.gnorm_bc, "c5", writes=[("gnorm",)])
        self.dma("sp", self.bs_t, self.bs_bc, "c6", writes=[("bs",)])
        self.dma("pool", self.wsT, self.w_sT.rearrange("p (g i) -> p g i", g=8), "c7", writes=[("wsT",)])
        self.dma("sp", self.bs_s_t, self.bs_bc_s, "c10", writes=[("bs_s",)])
        self.dma("pool", self.wsT_s, self.w_sT_s.rearrange("p (g i) -> p g i", g=8), "c11", writes=[("wsT_s",)])
        S.add("dve", lambda h: h.memset(self.eps_t, EPS), writes=[("eps",)])
        self.cp("dve", self.identb, self.ident, [("ident",)], [("identb",)])
        S.add("dve", lambda h: h.memset(self.onesb, 1.0), writes=[("onesb",)])
        self.act(self.scT, self.cvec_t, AF.Silu, [("cvec",)], [("scT",)])

    def adaln_prefetch(self):
        save = self.ptr
        self.ptr = ARENA_WORDS - 3 * 2048 - 8
        self.ada_slabs = [self.a_bf(4096).rearrange("p (c f) -> p c f", c=16) for _ in range(3)]
        self.ptr = save
        self.ada_next = 0
        self.ada_loaded = 0
        self.ada_limit = 56
        self.ada_psb = self.bank(7, 288)
        for _ in range(3):
            self._ada_load()

    def adaln_begin(self):
        self._ada_evac(0, 32)
        for s in range(2):
            m = self.modT[:, s, :].rearrange("p (m c) -> p m c", m=9)
            self.stt(self.coef[:, s, 0, :], m[:, 1, :], 1.0, self.gains_t[:, 0, :], ALU.add, ALU.mult,
                     [("modT",), ("gains",)], [("coef", s, 0)])

    def adaln_gate1(self):
        for _ in range(8):
            self.adaln_slabs(1)
            yield
        self._ada_evac(32, 48)
        for s in range(2):
            m = self.modT[:, s, :].rearrange("p (m c) -> p m c", m=9)
            self.ts(self.coef[:, s, 1, :], m[:, 2, :], 0.5, None, ALU.mult, None, [("modT",)], [("coef", s, 1)])
        yield

    def _ada_load(self):
        i = self.ada_loaded
        if i >= self.ada_limit:
            return
        self.ada_loaded += 1
        self.dma("pool", self.ada_slabs[i % 3], self.w_ada[i], "ada%d" % (i % 3),
                 writes=[("aslab", i % 3)])

    def adaln_slabs(self, n):
        for _ in range(n):
            i = self.ada_next
            if i >= self.ada_limit:
                return
            self.ada_next += 1
            for cc in range(2):
                j = 2 * i + cc
                for dc in range(16):
                    self.mm(self.ada_psb[:, 2 * j:2 * j + 2], self.ada_slabs[i % 3][:, dc, cc * 128:(cc + 1) * 128],
                            self.scT[:, 2 * dc:2 * dc + 2], dc == 0, dc == 15,
                            [("aslab", i % 3), ("scT",)], [("ps", 7)])
            self._ada_load()

    def _ada_evac(self, j0, j1):
        psv = self.ada_psb.rearrange("p (j s) -> p j s", s=2)
        for s in range(2):
            self.tt(self.modT[:, s, j0:j1], psv[:, j0:j1, s], self.bada_t[:, j0:j1], ALU.add,
                    [("ps", 7), ("bada",)], [("modT", j0)])

    def adaln_hook(self, fg):
        self.adaln_slabs(2 if (fg % 2 == 0 and fg < 20) else 1)

    def adaln_finish(self):
        self.adaln_slabs(72)
        self._ada_evac(48, 2 * self.ada_limit)
        for s in range(2):
            m = self.modT[:, s, :].rearrange("p (m c) -> p m c", m=9)
            self.stt(self.coef[:, s, 2, :], m[:, 4, :], 1.0, self.gains_t[:, 1, :], ALU.add, ALU.mult,
                     [("modT",), ("gains",)], [("coef", s, 2)])
        self.S.barrier()

    def adaln_resume(self):
        self.ada_limit = 72
        for _ in range(3):
            self._ada_load()

    def adaln_finish2(self):
        self.adaln_slabs(72)
        self._ada_evac(112, 144)
        for s in range(2):
            m = self.modT[:, s, :].rearrange("p (m c) -> p m c", m=9)
            self.stt(self.coef[:, s, 3, :], m[:, 7, :], 1.0, self.gains_t[:, 2, :], ALU.add, ALU.mult,
                     [("modT",), ("gains",)], [("coef", s, 3)])
            self.ts(self.coef[:, s, 4, :], m[:, 8, :], 0.5, None, ALU.mult, None, [("modT",)], [("coef", s, 4)])
        self.S.barrier()

    def mod(self, s, mi):
        return self.modT[:, s, mi * 16:(mi + 1) * 16]

    def normmod(self, src, dst, tiles, nch, Dn, scale_of, bias_of, srcname, dstname, tmps, psbanks):
        self.norm_stats(src, tiles, nch, Dn, srcname, tmps, psbanks)
        self.norm_apply(src, dst, tiles, nch, scale_of, bias_of, srcname, dstname, tmps)

    def norm_stats(self, src, tiles, nch, Dn, srcname, tmps, psbanks):
        sq, std, rstd, tmp = tmps
        assert len(tiles) <= 2
        for ti, (c0, n, tix) in enumerate(tiles):
            pb = psbanks[ti % len(psbanks)]
            pbank = self.bank(pb, n)
            ngr = nch // 4
            for g in range(ngr):
                sqb = sq[g % 2]
                self.act(sqb[:, :, 0:n], src[:, 4 * g:4 * g + 4, c0:c0 + n], AF.Square,
                         [(srcname, tix, c) for c in range(4 * g, 4 * g + 4)], [("sq", g % 2)])
                for r in range(4):
                    c = 4 * g + r
                    self.mm(pbank, self.onesb, sqb[:, r, 0:n], c == 0, c == nch - 1,
                            [("sq", g % 2), ("onesb",)], [("ps", pb)])
            self.act(std[ti][:, 0:n], pbank, AF.Sqrt, [("ps", pb)], [("std", ti)], scale=1.0 / Dn, bias=self.eps_t)
            self.recip(rstd[ti][:, 0:n], std[ti][:, 0:n], [("std", ti)], [("rstd", ti)])

    def norm_apply(self, src, dst, tiles, nch, scale_of, bias_of, srcname, dstname, tmps, extra_reads=()):
        sq, std, rstd, tmp = tmps
        for ti, (c0, n, tix) in enumerate(tiles):
            for c in range(nch):
                tb = tmp[c % 2]
                self.tt(tb[:, 0:n], src[:, c, c0:c0 + n], rstd[ti][:, 0:n], ALU.mult,
                        [(srcname, tix, c), ("rstd", ti)], [("ntmp", c % 2)])
                sc = scale_of(c, tix)
                bi = bias_of(c, tix) if bias_of is not None else None
                self.act(dst[:, c, c0:c0 + n], tb[:, 0:n], AF.Identity, [("ntmp", c % 2)] + list(extra_reads),
                         [(dstname, tix, c)], scale=sc, bias=bi)

    def ffn(self, G, Wg, Wu, Wd, tiles, set_of, gk, slabs, tmps, hook=None, dn_banks=(4, 5, 6, 7), first_gen=None,
            early_loads=None):
        wg, wu, wd = slabs
        sg, actb = tmps
        xT, hT = G["xT"], G["hT"]

        def load(fg, alias=()):
            b = fg % 2
            self.dma("pool", wg[b], Wg[fg], "wg%d" % b, writes=[("wg", b)] + [(a, b) for a in alias])
            self.dma("pool", wu[b], Wu[fg], "wu%d" % b, writes=[("wu", b)])
            self.dma("pool", wd[b], Wd[fg], "wd%d" % b, writes=[("wd", b)])

        if early_loads == "issue":
            load(0, alias=("stage",))
            load(1, alias=("stage",))
            return

        units = [(fg, t) for fg in range(NFG) for t in range(len(tiles))]
        state = {"it": 0, "dn": 0}

        def gateup(ui):
            fg, t = units[ui]
            c0, n, tix = tiles[t]
            b = fg % 2
            for fc in range(2):
                it = state["it"]
                state["it"] += 1
                pg, pu = it % 2, 2 + it % 2
                for dc in range(16):
                    self.mm(self.bank(pg, n), wg[b][:, dc, fc * 128:(fc + 1) * 128], hT[:, dc, c0:c0 + n],
                            dc == 0, dc == 15, [("wg", b), ("hT", tix, dc)], [("ps", pg)])
                    if dc % 4 == 3:
                        if dc == 15:
                            self.act(sg[it % 2][:, 0:n], self.bank(pg, n), AF.Silu, [("ps", pg)], [("sg", it % 2)])
                        yield
                for dc in range(16):
                    self.mm(self.bank(pu, n), wu[b][:, dc, fc * 128:(fc + 1) * 128], hT[:, dc, c0:c0 + n],
                            dc == 0, dc == 15, [("wu", b), ("hT", tix, dc)], [("ps", pu)])
                    if dc % 4 == 3:
                        if dc == 15:
                            self.tt(actb[ui % 2][:, fc, 0:n], sg[it % 2][:, 0:n], self.bank(pu, n), ALU.mult,
                                    [("sg", it % 2), ("ps", pu)], [("actb", ui % 2, fc)])
                        yield

        def down(ui):
            fg, t = units[ui]
            c0, n, tix = tiles[t]
            b = fg % 2
            s = set_of(tix)
            for dc in range(16):
                pd = dn_banks[state["dn"] % len(dn_banks)]
                state["dn"] += 1
                for fc in range(2):
                    self.mm(self.bank(pd, n), wd[b][:, fc, dc * 128:(dc + 1) * 128], actb[ui % 2][:, fc, 0:n],
                            fc == 0, fc == 1, [("wd", b), ("actb", ui % 2, fc)], [("ps", pd)])
                self.stt(xT[:, dc, c0:c0 + n], self.bank(pd, n), self.coef[:, s, gk, dc:dc + 1],
                         xT[:, dc, c0:c0 + n], ALU.mult, ALU.add,
                         [("ps", pd), ("xT", tix, dc), ("coef", s, gk)], [("xT", tix, dc)])
                if dc == 15 and t == len(tiles) - 1:
                    if fg + 2 < NFG:
                        load(fg + 2)
                    if hook is not None:
                        hook(fg)
                yield

        if early_loads != "done":
            load(0)
            load(1)
        for ui in range(len(units)):
            g = gateup(ui)
            d = down(ui - 1) if ui > 0 else None
            for ch in range(16):
                next(g)
                if ui == 0 and first_gen is not None and ch % 2 == 1:
                    next(first_gen, None)
                if d is not None and ch >= 4:
                    next(d)
                    if ch in (7, 10, 13, 15):
                        next(d)
            for _ in g:
                pass
            if ui == 0 and first_gen is not None:
                for _ in first_gen:
                    pass
            if d is not None:
                for _ in d:
                    pass
        for _ in down(len(units) - 1):
            pass

    def gelu(self, psrc, out, n, reads, writes, t1, t2, k):
        self.act(t1, psrc, AF.Square, reads, [("gt1", k)])
        self.ts(t1, t1, 0.044715, 1.0, ALU.mult, ALU.add, [("gt1", k)], [("gt1", k)])
        self.tt(t2, t1, psrc, ALU.mult, [("gt1", k)] + list(reads), [("gt2", k)])
        self.act(t2, t2, AF.Sigmoid, [("gt2", k)], [("gt2", k)], scale=1.5957691216057308)
        self.tt(out, t2, psrc, ALU.mult, [("gt2", k)] + list(reads), writes)

    def run_group(self, name, gbase):
        S = self.S
        isS = (name == "S")
        if not isS:
            n_all, n_own = 768, 512
            tiles_all = [(0, 512, 0), (512, 256, 1)]
            tiles_own = [(0, 512, 0)]
            set_of = lambda tix: 0 if tix == 0 else 1
        else:
            n_all, n_own = 1024, 1024
            tiles_all = [(0, 512, 0), (512, 512, 1)]
            tiles_own = tiles_all
            set_of = lambda tix: 1
        nsub_all, nsub_own = n_all // 128, n_own // 128
        G = {}
        self.ptr = gbase
        r0 = self.ptr
        hT = self.a_bf(16 * n_all).rearrange("p (c t) -> p c t", c=16)
        r1 = self.ptr
        xT = self.a_f32(16 * n_all).rearrange("p (c t) -> p c t", c=16)
        G["xT"], G["hT"] = xT, hT
        pbase = self.ptr

        def xrows(ts):
            if isS:
                return self.xs[ts * 128:(ts + 1) * 128, :]
            if ts < 4:
                return self.xp[ts * 128:(ts + 1) * 128, :]
            return self.xs[1024 + (ts - 4) * 128:1024 + (ts - 3) * 128, :]

        def norm_tmps():
            sq = [self.a_bf(4 * 512).rearrange("p (r t) -> p r t", r=4) for _ in range(2)]
            std = [self.a_f32(512) for _ in range(2)]
            rstd = [self.a_f32(512) for _ in range(2)]
            tmp = [self.a_f32(512) for _ in range(2)]
            return sq, std, rstd, tmp

        def ffn_bufs():
            wg = [self.a_bf(4096).rearrange("p (c f) -> p c f", c=16) for _ in range(2)]
            wu = [self.a_bf(4096).rearrange("p (c f) -> p c f", c=16) for _ in range(2)]
            wd = [self.a_bf(4096).rearrange("p (fc d) -> p fc d", fc=2) for _ in range(2)]
            sg = [self.a_f32(512) for _ in range(2)]
            actb = [self.a_bf(1024).rearrange("p (fc t) -> p fc t", fc=2) for _ in range(2)]
            return (wg, wu, wd), (sg, actb)

        if not isS:
            self.adaln_prefetch()
        nt = norm_tmps()
        fbase = self.ptr
        stage = [self.a_f32(2048) for _ in range(2)]
        for ts in range(nsub_all):
            sb = ts % 2
            tix = ts // 4
            self.dma("sp", stage[sb], xrows(ts), "xst%d" % sb, writes=[("stage", sb)])
            for q in range(4):
                pb = 4 * sb + q
                for r in range(4):
                    dc = 4 * q + r
                    self.tr(self.bank(pb)[:, r * 128:(r + 1) * 128], stage[sb][:, dc * 128:(dc + 1) * 128], self.ident,
                            [("stage", sb), ("ident",)], [("ps", pb)])
                self.cp(self.evac_engine(), xT[:, 4 * q:4 * q + 4, ts * 128:(ts + 1) * 128],
                        self.bank(pb).rearrange("p (r t) -> p r t", r=4),
                        [("ps", pb)], [("xT", tix, 4 * q + r) for r in range(4)])
        self.ptr = fbase
        slabs, ftmps = ffn_bufs()

        if not isS:
            self.adaln_slabs(16)
        self.norm_stats(xT, tiles_all, 16, D, "xT", nt, [4, 5])
        if not isS:
            self.ffn(G, self.w1g, self.w1u, self.w1d, tiles_all, set_of, 1, slabs, ftmps, early_loads="issue")
            self.adaln_begin()
            xr = [("coef",), ("modT",)]
        else:
            self.ffn(G, self.w1g, self.w1u, self.w1d, tiles_all, set_of, 1, slabs, ftmps, early_loads="issue")
            xr = []
        self.norm_apply(xT, hT, tiles_all, 16,
                        lambda c, tix: self.coef[:, set_of(tix), 0, c:c + 1],
                        lambda c, tix: self.mod(set_of(tix), 0)[:, c:c + 1],
                        "xT", "hT", nt, extra_reads=xr)
        if self.debug and self.debug.get("stop") == name + ":norm1":
            return self.dump(hT)
        if not isS:
            self.ffn(G, self.w1g, self.w1u, self.w1d, tiles_all, set_of, 1, slabs, ftmps,
                     hook=self.adaln_hook, dn_banks=(4, 5, 6), first_gen=self.adaln_gate1(), early_loads="done")
            self.adaln_finish()
        else:
            self.ffn(G, self.w1g, self.w1u, self.w1d, tiles_all, set_of, 1, slabs, ftmps, early_loads="done")
        if self.debug and self.debug.get("stop") == name + ":ffn1":
            return self.dump(xT)

        if not isS:
            save = self.ptr
            self.ptr = pbase + 5 * (8 * n_own) // 2
            ws_pre = [self.a_bf(4096).rearrange("p (c f) -> p c f", c=16) for _ in range(4)]
            self.ptr = save
            for si in range(4):
                self.dma("pool", ws_pre[si], self.w_in[si], "ws%d" % si, writes=[("ws", si)])
        self.normmod(xT, hT, tiles_all, 16, D,
                     lambda c, tix: self.coef[:, set_of(tix), 2, c:c + 1],
                     lambda c, tix: self.mod(set_of(tix), 3)[:, c:c + 1],
                     "xT", "hT", nt, [4, 5])
        if isS:
            self.dma("sp", self.xscr, self.arena_flat(xT), "spill", reads=[("xT",)], writes=[("xscr",)])
        S.barrier()
        self.ptr = r1 if isS else pbase
        kT = self.a_bf(8 * n_own).rearrange("p (c t) -> p c t", c=8)
        v = self.a_bf(nsub_own * 1024).rearrange("p (s c) -> p s c", s=nsub_own)
        qT = self.a_bf(8 * n_own).rearrange("p (c t) -> p c t", c=8)
        mbase = self.ptr
        guT = self.a_bf(8 * n_own).rearrange("p (c t) -> p c t", c=8)
        vn = self.a_bf(nsub_own * 1024).rearrange("p (s c) -> p s c", s=nsub_own)
        gbase2 = self.ptr
        ws = [self.a_bf(4096).rearrange("p (c f) -> p c f", c=16) for _ in range(4)]
        gt1 = [self.a_f32(512) for _ in range(2)]
        gt2 = [self.a_f32(512) for _ in range(2)]
        gv = [self.a_f32(1024) for _ in range(2)]
        gsq = self.a_f32(1024)
        gss = self.a_f32(8)
        gstd = self.a_f32(8)
        grs = self.a_f32(8)
        ost = [self.a_f32(256) for _ in range(2)]

        def wload(si):
            self.dma("pool", ws[si % 4], self.w_in[si], "ws%d" % (si % 4), writes=[("ws", si % 4)])

        if isS:
            for si in range(4):
                wload(si)
        else:
            assert all(ws[k].offset == ws_pre[k].offset for k in range(4))
        fmb = [0]
        tmb = [0]
        gk = [0]
        for si in range(16):
            typ, w = si // 4, si % 4
            sl = ws[si % 4]
            rs = [("ws", si % 4)]
            if typ in (0, 1, 3):
                ttiles = tiles_all if typ == 1 else tiles_own
                for (c0, n, tix) in ttiles:
                    for cc in range(2):
                        pb = fmb[0] % 4
                        fmb[0] += 1
                        for dc in range(16):
                            self.mm(self.bank(pb, n), sl[:, dc, cc * 128:(cc + 1) * 128], hT[:, dc, c0:c0 + n],
                                    dc == 0, dc == 15, rs + [("hT", tix, dc)], [("ps", pb)])
                        ch = 2 * w + cc
                        if typ == 0:
                            self.cp(self.evac_engine(), qT[:, ch, c0:c0 + n], self.bank(pb, n), [("ps", pb)], [("qT", ch, tix)])
                        elif typ == 1:
                            if (not isS) and tix == 1:
                                dstk, wr = self.kT_halo[:, ch, 0:n], [("kTh", ch)]
                            else:
                                dstk, wr = kT[:, ch, c0:c0 + n], [("kT", ch, tix)]
                            self.cp(self.evac_engine(), dstk, self.bank(pb, n), [("ps", pb)], wr)
                        else:
                            k = gk[0] % 2
                            gk[0] += 1
                            self.gelu(self.bank(pb, n), guT[:, ch, c0:c0 + n], n, [("ps", pb)], [("guT", ch, tix)],
                                      gt1[k][:, 0:n], gt2[k][:, 0:n], k)
            if typ in (1, 2):
                if typ == 1 and isS:
                    subs = []
                elif typ == 1:
                    subs = list(range(4))
                else:
                    subs = list(range(nsub_all))
                for ts in subs:
                    pb = 4 + tmb[0] % 4
                    tmb[0] += 1
                    tix = ts // 4
                    for dc in range(16):
                        self.mm(self.bank(pb, 256), hT[:, dc, ts * 128:(ts + 1) * 128], sl[:, dc, :],
                                dc == 0, dc == 15, rs + [("hT", tix, dc)], [("ps", pb)])
                    if typ == 2:
                        if (not isS) and ts >= 4:
                            dstv, wr = self.v_halo[:, ts - 4, w * 256:(w + 1) * 256], [("vh", ts - 4, w)]
                        else:
                            dstv, wr = v[:, ts, w * 256:(w + 1) * 256], [("v", ts, w)]
                        self.cp(self.evac_engine(), dstv, self.bank(pb, 256), [("ps", pb)], wr)
                    if (not isS) and ts < 4:
                        ob = tmb[0] % 2
                        self.cp(self.evac_engine(), ost[ob], self.bank(pb, 256), [("ps", pb)], [("ost", ob)])
                        dst = (self.sk if typ == 1 else self.sv)[ts * 128:(ts + 1) * 128, w * 256:(w + 1) * 256]
                        self.dma("sp", dst, ost[ob], "ost%d" % ob, reads=[("ost", ob)])
            if si + 4 < 20:
                wload(si + 4)
        gnv = self.gnorm_t
        for ts in range(nsub_own):
            tix = ts // 4
            k = ts % 2
            for half in range(2):
                pb = 4 + tmb[0] % 4
                tmb[0] += 1
                for hh in range(2):
                    si = 16 + 2 * half + hh
                    for dc in range(16):
                        self.mm(self.bank(pb)[:, hh * 256:(hh + 1) * 256], hT[:, dc, ts * 128:(ts + 1) * 128],
                                ws[si % 4][:, dc, :], dc == 0, dc == 15,
                                [("ws", si % 4), ("hT", tix, dc)], [("ps", pb)])
                self.gelu(self.bank(pb), gv[k][:, half * 512:(half + 1) * 512], 512, [("ps", pb)], [("gv", k, half)],
                          gt1[half][:, 0:512], gt2[half][:, 0:512], half)
            self.act(gsq, gv[k], AF.Square, [("gv", k)], [("gsq",)])
            S.add("dve", lambda h, o=gss, i=gsq: h.tensor_reduce(o, i.rearrange("p (g c) -> p g c", g=8),
                                                                  mybir.AxisListType.X, ALU.add),
                  reads=[("gsq",)], writes=[("gss",)])
            self.act(gstd, gss, AF.Sqrt, [("gss",)], [("gstd",)], scale=1.0 / 128, bias=self.eps_t)
            self.recip(grs, gstd, [("gstd",)], [("grs",)])
            g3 = gv[k].rearrange("p (g c) -> p g c", g=8)
            self.tt(g3, g3, grs.unsqueeze(2).broadcast_to([128, 8, 128]), ALU.mult,
                    [("gv", k), ("grs",)], [("gv", k)])
            self.tt(vn[:, ts, :], gv[k], gnv, ALU.mult, [("gv", k), ("gnorm",)], [("vn", ts)])
        if self.debug and self.debug.get("stop") == name + ":mixin":
            which = self.debug.get("which", "qT")
            return self.dump({"qT": qT, "kT": kT, "v": v, "guT": guT, "vn": vn}[which])

        S.barrier()
        self.ptr = r0
        mix = self.a_bf(16 * n_own).rearrange("p (c t) -> p c t", c=16)
        assert self.ptr <= r1
        self.ptr = gbase2
        gtmp = [self.a_f32(512) for _ in range(2)]
        if isS:
            self.attn_sample_prep()
        else:
            self.adaln_resume()
        bs_use = self.bs_s_t if isS else self.bs_t
        ws_use = self.wsT_s if isS else self.wsT
        wsname = "wsT_s" if isS else "wsT"
        bsname = "bs_s" if isS else "bs"
        gi = 0
        for ts in range(nsub_own):
            tix = ts // 4
            for half in range(2):
                pb = gi % 4
                k = gi % 2
                gi += 1
                for q in range(4):
                    g = half * 4 + q
                    self.mm(self.bank(pb)[:, q * 128:(q + 1) * 128], vn[:, ts, g * 128:(g + 1) * 128], ws_use[:, g, :],
                            True, True, [("vn", ts), (wsname,)], [("ps", pb)])
                self.tt(gtmp[k], self.bank(pb), bs_use[:, half * 512:(half + 1) * 512], ALU.add,
                        [("ps", pb), (bsname,)], [("gtmp", k)])
                self.tt(mix[:, 8 + 4 * half:12 + 4 * half, ts * 128:(ts + 1) * 128],
                        gtmp[k].rearrange("p (g i) -> p g i", g=4),
                        guT[:, 4 * half:4 * half + 4, ts * 128:(ts + 1) * 128], ALU.mult,
                        [("gtmp", k)] + [("guT", 4 * half + q, tix) for q in range(4)],
                        [("mix", 8 + 4 * half + q, tix) for q in range(4)])
        S.barrier()
        self.ptr = mbase

        if not isS:
            self.attn_prompt(qT, kT, v, mix)
        else:
            self.attn_sample(qT, kT, v, mix)
        if self.debug and self.debug.get("stop") == name + ":attn":
            return self.dump(mix)
        S.barrier()
        self.ptr = pbase

        if isS:
            self.dma("sp", self.arena_flat(xT), self.xscr, "reload", reads=[("xscr",)], writes=[("xT",)])
        nt = norm_tmps()
        wo = [self.a_bf(4096).rearrange("p (c f) -> p c f", c=16) for _ in range(4)]

        def oload(i):
            self.dma("pool", wo[i % 4], self.w_out[i], "wo%d" % (i % 4), writes=[("wo", i % 4)])

        for i in range(4):
            oload(i)
        sset = 1 if isS else 0
        for half in range(2):
            self.normmod(mix[:, 8 * half:8 * half + 8, :], mix[:, 8 * half:8 * half + 8, :], tiles_own, 8, 1024,
                         lambda c, tix, half=half: self.gmix_t[:, 8 * half + c:8 * half + c + 1], None,
                         "mix%d" % half, "mix%d" % half, nt, [5, 6])
        ob = 0
        for i in range(8):
            for (c0, n, tix) in tiles_own:
                for cc in range(2):
                    pb = ob % 4
                    ob += 1
                    dcx = 2 * i + cc
                    for mc in range(16):
                        self.mm(self.bank(pb, n), wo[i % 4][:, mc, cc * 128:(cc + 1) * 128], mix[:, mc, c0:c0 + n],
                                mc == 0, mc == 15, [("wo", i % 4), ("mix%d" % (mc // 8), tix, mc % 8)], [("ps", pb)])
                    self.stt(xT[:, dcx, c0:c0 + n], self.bank(pb, n), self.mod(sset, 5)[:, dcx:dcx + 1],
                             xT[:, dcx, c0:c0 + n], ALU.mult, ALU.add,
                             [("ps", pb), ("xT", tix, dcx), ("modT",)], [("xT", tix, dcx)])
            if i + 4 < 8:
                oload(i + 4)
            if not isS:
                self.adaln_slabs(1)
        if self.debug and self.debug.get("stop") == name + ":mixout":
            return self.dump(xT)
        S.barrier()
        self.ptr = pbase

        nt = norm_tmps()
        slabs, ftmps = ffn_bufs()
        if not isS:
            self.adaln_finish2()
        self.normmod(xT, hT, tiles_own, 16, D,
                     lambda c, tix: self.coef[:, sset, 3, c:c + 1],
                     lambda c, tix: self.mod(sset, 6)[:, c:c + 1],
                     "xT", "hT", nt, [4, 5])
        self.ffn(G, self.w2g, self.w2u, self.w2d, tiles_own, lambda tix: sset, 4, slabs, ftmps)
        wg_ = slabs[0]
        ostage = [wg_[sb].rearrange("p c f -> p (c f)").bitcast(F32) for sb in range(2)]
        ydst = self.ys if isS else self.yp
        self.norm_stats(xT, tiles_own, 16, D, "xT", nt, [4, 5])
        for ti, (c0, n, tix) in enumerate(tiles_own):
            self.norm_apply(xT, xT, [tiles_own[ti]], 16,
                            lambda c, tix_: self.gains_t[:, 3, c:c + 1], None, "xT", "xT",
                            (nt[0], nt[1][ti:ti + 1], nt[2][ti:ti + 1], nt[3]))
            for ts in range(c0 // 128, (c0 + n) // 128):
                sb = ts % 2
                for q in range(4):
                    pb = 4 * sb + q
                    for r in range(4):
                        dc = 4 * q + r
                        self.tr(self.bank(pb)[:, r * 128:(r + 1) * 128], xT[:, dc, ts * 128:(ts + 1) * 128], self.ident,
                                [("xT", tix, dc), ("ident",)], [("ps", pb)])
                    self.cp(self.evac_engine(), ostage[sb][:, q * 512:(q + 1) * 512], self.bank(pb),
                            [("ps", pb)], [("ostage", sb, q), ("wg", sb)])
                self.dma("sp", ydst[ts * 128:(ts + 1) * 128, :], ostage[sb], "yst%d" % sb,
                         reads=[("ostage", sb), ("wg", sb)])
        S.barrier()
        return True

    def arena_flat(self, view3):
        return view3.rearrange("p c t -> p (c t)")

    def attn_prompt(self, qT, kT, v, mix):
        E = [self.a_bf(512) for _ in range(4)]
        rden = [self.a_f32(256) for _ in range(2)]
        jobs = [(s, w, hh) for s in range(2) for w in range(8) for hh in range(2)]

        def stage_a(it):
            s, w, hh = jobs[it]
            off = 64 * hh
            pb = it % 4
            for kc in range(2):
                tsb = 2 * s + kc
                self.mm(self.bank(pb)[:, kc * 256:(kc + 1) * 256], kT[off:off + 64, w, tsb * 128:(tsb + 1) * 128],
                        qT[off:off + 64, w, s * 256:(s + 1) * 256], True, True,
                        [("kT", w, 0), ("qT", w, 0)], [("ps", pb)])
            self.act(E[pb], self.bank(pb), AF.Exp, [("ps", pb)], [("E", pb)], scale=0.125)

        def stage_b(it):
            s, w, hh = jobs[it]
            po = 4 + (it % 2)
            eb = E[it % 4]
            for kc in range(2):
                tsb = 2 * s + kc
                self.mm(self.bank(po)[:, 0:256], v[:, tsb, w * 128:(w + 1) * 128], eb[:, kc * 256:(kc + 1) * 256],
                        kc == 0, kc == 1, [("v", tsb), ("E", it % 4)], [("ps", po)])
            for kc in range(2):
                self.mm(self.bank(po)[:, 256:512], self.onesb, eb[:, kc * 256:(kc + 1) * 256], kc == 0, kc == 1,
                        [("onesb",), ("E", it % 4)], [("ps", po)])
            if it % 3 == 2 and it >= 5:
                self.adaln_slabs(1)

        def stage_c(it):
            s, w, hh = jobs[it]
            off = 64 * hh
            po = 4 + (it % 2)
            rd = rden[it % 2]
            self.act(rd[off:off + 64], self.bank(po)[off:off + 64, 256:512], AF.Ln, [("ps", po)], [("rden", it % 2)])
            self.act(rd[off:off + 64], rd[off:off + 64], AF.Exp, [("rden", it % 2)], [("rden", it % 2)], scale=-1.0)
            self.tt(mix[off:off + 64, w, s * 256:(s + 1) * 256], self.bank(po)[off:off + 64, 0:256],
                    rd[off:off + 64], ALU.mult, [("ps", po), ("rden", it % 2)], [("mix", w, 0, hh, s)])

        n = len(jobs)
        for it in range(n + 3):
            if it < n:
                stage_a(it)
            if 0 <= it - 2 < n:
                stage_b(it - 2)
            if 0 <= it - 3 < n:
                stage_c(it - 3)

    def attn_sample_prep(self):
        S = self.S
        ckT = self.a_bf(8 * 512).rearrange("p (c t) -> p c t", c=8)
        cvb = self.a_bf(4 * 1024).rearrange("p (s c) -> p s c", s=4)
        cst = [self.a_f32(1024) for _ in range(2)]
        kTz = [self.a_bf(1280) for _ in range(2)]
        qTz = [self.a_bf(1024) for _ in range(2)]
        ckTz = [self.a_bf(512) for _ in range(2)]
        for hh in range(2):
            for buf, nm in ((kTz, "kTz"), (qTz, "qTz"), (ckTz, "ckTz")):
                S.add("dve", lambda h, b=buf[hh]: h.memset(b, 0.0), writes=[(nm, hh)])
            o2 = 64 * (1 - hh)
            self.dma("pool", kTz[hh][o2:o2 + 28], self.ind, "ind%d" % hh, writes=[("kTz", hh)])
            self.dma("pool", qTz[hh][o2:o2 + 28], self.rnz, "rnz%d" % hh, writes=[("qTz", hh)])
        self.dma("pool", cvb, self.cv.rearrange("(s p) c -> p s c", p=128), "cvb", writes=[("cvb",)])
        for ts in range(4):
            sb = ts % 2
            self.dma("sp", cst[sb], self.ck[ts * 128:(ts + 1) * 128, :], "cst%d" % sb, writes=[("cst", sb)])
            for q in range(2):
                pb = 4 + 2 * sb + q
                for r in range(4):
                    c = 4 * q + r
                    self.tr(self.bank(pb)[:, r * 128:(r + 1) * 128], cst[sb][:, c * 128:(c + 1) * 128], self.ident,
                            [("cst", sb), ("ident",)], [("ps", pb)])
                self.cp(self.evac_engine(), ckT[:, 4 * q:4 * q + 4, ts * 128:(ts + 1) * 128],
                        self.bank(pb).rearrange("p (r t) -> p r t", r=4), [("ps", pb)],
                        [("ckT", 4 * q + r, ts) for r in range(4)])
        self.attn_bufs = (ckT, cvb, kTz, qTz, ckTz)

    def attn_sample(self, qT, kT, v, mix):
        S = self.S
        ckT, cvb, kTz, qTz, ckTz = self.attn_bufs
        ebst = self.a_f32(1472)
        eb = [self.a_bf(1472) for _ in range(2)]
        E = [self.a_bf(512) for _ in range(3)]
        PT = [self.a_bf(512) for _ in range(4)]
        rden = [self.a_f32(512) for _ in range(2)]
        chunks = {0: [0, 2, 4, 6, 8, 10], 1: [4, 6, 8, 10, 12, 14, 16, 18]}
        posof = {}
        p = 0
        for t in (0, 1):
            for l0 in chunks[t]:
                posof[(t, l0)] = p
                p += 1
        def head_copies(hd):
            w_, hh_ = hd // 2, hd % 2
            o_ = 64 * hh_
            self.cp("dve", kTz[hh_][o_:o_ + 64, 0:1024], kT[o_:o_ + 64, w_, :], [("kT", w_)], [("kTz", hh_)])
            self.cp("dve", kTz[hh_][o_:o_ + 64, 1024:1280], self.kT_halo[o_:o_ + 64, w_, :], [("kTh", w_)], [("kTz", hh_)])
            self.cp("dve", qTz[hh_][o_:o_ + 64, :], qT[o_:o_ + 64, w_, :], [("qT", w_)], [("qTz", hh_)])
            self.cp("dve", ckTz[hh_][o_:o_ + 64, :], ckT[o_:o_ + 64, w_, :], [("ckT", w_)], [("ckTz", hh_)])

        def _valid(par, l, j):
            gj = j if par == 0 else 31 - j
            gl = l if par == 0 else 31 - l
            rs = min(max(gj - 4, 0), 24)
            return rs <= gl <= rs + 7

        jrange = {}
        for t_ in (0, 1):
            for l0_ in chunks[t_]:
                js = [jj for jj in range(8) if any(_valid(p_, l0_ + a_, 8 * t_ + jj) for p_ in (0, 1) for a_ in (0, 1))]
                jrange[(t_, l0_)] = (min(js), max(js) + 1)

        def head_eb(hd):
            self.dma("sp", ebst, self.biasT[hd], "ebst", writes=[("ebst",)])
            self.act(eb[hd % 2], ebst, AF.Exp, [("ebst",)], [("eb", hd % 2)])

        def qk(job, cidx, ci):
            w, hh, t, h, off, q0, po, pd, nch, it = job
            ebb = eb[h % 2]
            pb = ci % 4
            ptb = PT[ci % 4]
            if cidx >= 4:
                l0 = chunks[t][cidx - 4]
                ja, jb = jrange[(t, l0)]
                ca, cb = ja * 64, jb * 64
                if l0 < 16:
                    vsrc = v[:, l0 // 2, w * 128:(w + 1) * 128]
                    vreg = ("v", l0 // 2)
                else:
                    vsrc = self.v_halo[:, (l0 - 16) // 2, w * 128:(w + 1) * 128]
                    vreg = ("vh", (l0 - 16) // 2)
                self.mm(self.bank(pb)[:, ca:cb], kTz[hh][:, l0 * 64:l0 * 64 + 128], qTz[hh][:, q0 + ca:q0 + cb], True, True,
                        [("kTz", hh), ("qTz", hh)], [("ps", pb)])
                ebuf = E[ci % 3]
                self.act(ebuf[:, ca:cb], self.bank(pb)[:, ca:cb], AF.Exp, [("ps", pb)], [("E", ci % 3)], scale=0.125)
                ei0 = 8 * t - l0 + 11
                self.tt(ptb[:, ca:cb], ebuf[:, ca:cb], ebb[:, (ei0 + ja) * 64:(ei0 + jb) * 64], ALU.mult,
                        [("E", ci % 3), ("eb", h % 2)], [("PT", ci % 4)])
            else:
                cc = cidx
                ca, cb = 0, 512
                self.mm(self.bank(pb), ckTz[hh][:, cc * 128:(cc + 1) * 128], qTz[hh][:, q0:q0 + 512], True, True,
                        [("ckTz", hh), ("qTz", hh)], [("ps", pb)])
                self.act(ptb, self.bank(pb), AF.Exp, [("ps", pb)], [("PT", ci % 4)], scale=0.125)
                vsrc = cvb[:, cc, w * 128:(w + 1) * 128]
                vreg = ("cvb",)
            return (vsrc, vreg, ptb, ci % 4, ca, cb)

        def pv(job, cidx, st):
            w, hh, t, h, off, q0, po, pd, nch, it = job
            vsrc, vreg, ptb, pti, ca, cb = st
            self.mm(self.bank(po)[:, ca:cb], vsrc, ptb[:, ca:cb], cidx == 0, cidx == nch - 1,
                    [vreg, ("PT", pti)], [("ps", po)])
            self.mm(self.bank(pd)[:, ca:cb], self.onesb, ptb[:, ca:cb], cidx == 0, cidx == nch - 1,
                    [("onesb",), ("PT", pti)], [("ps", pd)])

        def finalize(job):
            w, hh, t, h, off, q0, po, pd, nch, it = job
            rd = rden[it % 2]
            self.act(rd[off:off + 64], self.bank(pd)[off:off + 64], AF.Ln, [("ps", pd)], [("rden", it % 2)])
            self.act(rd[off:off + 64], rd[off:off + 64], AF.Exp, [("rden", it % 2)], [("rden", it % 2)], scale=-1.0)
            self.tt(mix[off:off + 64, w, q0:q0 + 512], self.bank(po)[off:off + 64], rd[off:off + 64], ALU.mult,
                    [("ps", po), ("rden", it % 2)], [("mix", w, t, hh)])

        head_copies(0)
        head_eb(0)
        steps = []
        it = 0
        for h in range(16):
            w, hh = h // 2, h % 2
            for t in range(2):
                nch = len(chunks[t]) + 4
                job = (w, hh, t, h, 64 * hh, t * 512, 4 + (it % 2), 6 + (it % 2), nch, it)
                it += 1
                for cidx in range(nch):
                    steps.append((job, cidx))
        staged = {}

        def retire(k):
            jb, cb = steps[k]
            pv(jb, cb, staged.pop(k))
            if cb == jb[8] - 1:
                finalize(jb)

        for k, (job, cidx) in enumerate(steps):
            if cidx == 0 and job[2] == 1 and job[3] + 1 < 16:
                head_copies(job[3] + 1)
                head_eb(job[3] + 1)
            staged[k] = qk(job, cidx, k)
            if k >= 2:
                retire(k - 2)
        retire(len(steps) - 2)
        retire(len(steps) - 1)

    def dump(self, view):
        S = self.S
        S.barrier()
        n = self.debug["n"]
        shp = view.shape
        if len(shp) == 3:
            flat = view.rearrange("p c t -> p (c t)")
        else:
            flat = view
        if flat.dtype != F32:
            self.ptr = ARENA_WORDS - ((n + 7) // 8 * 8) - 8
            tmp = self.a_f32(n)
            self.cp("dve", tmp, flat[:, 0:n], [], [("dbgtmp",)])
            flat = tmp
        self.dma("sp", self.dbg, flat[:, 0:n], "dbg", reads=[("dbgtmp",)])
        return False


def _fm(vec, nch):
    return np.ascontiguousarray(np.asarray(vec, np.float32).reshape(nch, 128).T)


def _bias_table(rpb_l, par):
    kc = np.arange(64)[:, None]
    qc = np.arange(64)[None, :]
    dcidx = np.clip(kc - qc + 15, 0, 30)
    cs = np.clip(qc - 8, 0, 48)
    colvalid = (kc >= cs) & (kc < cs + 16)
    T = np.zeros((16, 64, 23, 64), np.float32)
    for ei in range(23):
        e = ei - 11
        if abs(e) <= 7:
            dr = -e if par == 0 else e
            vals = rpb_l[:, dr + 7][:, dcidx]
            T[:, :, ei, :] = np.where(colvalid[None], vals, np.float32(NEG))
    T2 = np.zeros_like(T)
    T2[:, :, 1:, :] = T[:, :, :-1, :]
    T = np.concatenate([T, T2], axis=1)
    return np.ascontiguousarray(T.reshape(16, 128, 23 * 64))


def _masks(par):
    chunks = {0: [0, 2, 4, 6, 8, 10], 1: [4, 6, 8, 10, 12, 14, 16, 18]}
    ind = np.zeros((28, 1280), np.float32)
    rnz = np.zeros((28, 1024), np.float32)
    pos = 0
    for t in (0, 1):
        for l0 in chunks[t]:
            for a in range(2):
                l = l0 + a
                ind[2 * pos + a, l * 64:(l + 1) * 64] = 1.0
                for jj in range(8):
                    j = 8 * t + jj
                    gj = j if par == 0 else 31 - j
                    gl = l if par == 0 else 31 - l
                    rs = min(max(gj - 4, 0), 24)
                    valid = rs <= gl <= rs + 7
                    rnz[2 * pos + a, j * 64:(j + 1) * 64] = 0.0 if valid else NEG
            pos += 1
    return ind, rnz


_NC_CACHE = {}


def _get_nc(debug=None):
    key = None if debug is None else tuple(sorted(debug.items()))
    if key not in _NC_CACHE:
        _NC_CACHE[key] = Builder(debug).build()
    return _NC_CACHE[key]


def make_in_maps(x_prompt, x_sample, cache_k, cache_v, c, c_ctx, w_ada, b_ada,
                 ffn1_norm, ffn1_w_gate, ffn1_w_up, ffn1_w_down,
                 mix_norm, w_in, rpb, gmlp_norm, w_s, b_s, out_norm_a, out_norm_b, w_out,
                 ffn2_norm, ffn2_w_gate, ffn2_w_up, ffn2_w_down, final_norm):
    f = lambda a: np.ascontiguousarray(np.asarray(a, np.float32))

    def kslab(w):
        w = np.asarray(w, np.float32)
        n = w.shape[1] // 256
        return np.ascontiguousarray(w.reshape(16, 128, n, 256).transpose(2, 1, 0, 3))

    def dslab(w):
        w = np.asarray(w, np.float32)
        return np.ascontiguousarray(w.reshape(NFG, 2, 128, D).transpose(0, 2, 1, 3))
    shared = {
        "w_ada": kslab(w_ada[0]),
        "b_ada_fm": _fm(b_ada[0], 144),
        "gains": np.ascontiguousarray(np.concatenate(
            [_fm(ffn1_norm[0], 16), _fm(mix_norm[0], 16), _fm(ffn2_norm[0], 16), _fm(final_norm, 16)], axis=1)),
        "gains_mix": np.ascontiguousarray(np.concatenate([_fm(out_norm_a[0], 8), _fm(out_norm_b[0], 8)], axis=1)),
        "gnorm_bc": np.ascontiguousarray(np.broadcast_to(np.asarray(gmlp_norm[0], np.float32)[None, :], (128, 1024))),
        "bs_bc": np.ascontiguousarray(np.broadcast_to(np.asarray(b_s[0], np.float32).reshape(1, 1024), (128, 1024))),
        "w_sT": np.ascontiguousarray(np.asarray(w_s[0], np.float32).transpose(2, 0, 1).reshape(128, 1024)),
        "w1g": kslab(ffn1_w_gate[0]), "w1u": kslab(ffn1_w_up[0]), "w1d": dslab(ffn1_w_down[0]),
        "w2g": kslab(ffn2_w_gate[0]), "w2u": kslab(ffn2_w_up[0]), "w2d": dslab(ffn2_w_down[0]),
        "w_in": kslab(w_in[0]), "w_out": kslab(w_out[0]),
        "ident": np.eye(128, dtype=np.float32),
    }
    rpb_l = np.asarray(rpb[0], np.float32)
    par_tabs = {}
    for par in (0, 1):
        ind, rnz = _masks(par)
        par_tabs[par] = (_bias_table(rpb_l, par), ind, rnz)
    in_maps = []
    for core in range(N_CORES):
        b, par = core // 2, core % 2
        xs_full = np.asarray(x_sample[b], np.float32).reshape(32, 64, D)
        if par == 1:
            xs_full = xs_full[::-1]
        xs = np.ascontiguousarray(xs_full[0:20].reshape(1280, D))
        cv2 = np.stack([_fm(c_ctx, 16), _fm(c[b], 16)], axis=2).reshape(128, 32)
        ws_l = np.asarray(w_s[0], np.float32)
        bs_l = np.asarray(b_s[0], np.float32)
        if par == 1:
            perm = np.concatenate([np.arange(64, 128), np.arange(0, 64)])
            ws_l = ws_l[:, perm][:, :, perm]
            bs_l = bs_l[:, perm]
        m = dict(shared)
        m.update({
            "w_sT_s": np.ascontiguousarray(ws_l.transpose(2, 0, 1).reshape(128, 1024)),
            "bs_bc_s": np.ascontiguousarray(np.broadcast_to(bs_l.reshape(1, 1024), (128, 1024))),
            "xp": np.ascontiguousarray(np.asarray(x_prompt[2 * core:2 * core + 2], np.float32).reshape(512, D)),
            "xs": xs,
            "ck": np.ascontiguousarray(np.asarray(cache_k[b, 0], np.float32).reshape(512, 1024)),
            "cv": np.ascontiguousarray(np.asarray(cache_v[b, 0], np.float32).reshape(512, 1024)),
            "cvec": np.ascontiguousarray(cv2),
            "biasT": par_tabs[par][0], "ind": par_tabs[par][1], "rnz": par_tabs[par][2],
        })
        in_maps.append(m)
    return in_maps


def kernel(**inputs):
    nc = _get_nc()
    in_maps = make_in_maps(**inputs)
    res = run_bass_kernel_spmd(nc, in_maps, core_ids=list(range(N_CORES)))
    y_prompt = np.zeros((16, 256, D), np.float32)
    y_sample = np.zeros((4, 2048, D), np.float32)
    state_k = np.zeros((16, 1, 256, 16, 64), np.float32)
    state_v = np.zeros((16, 1, 256, 16, 64), np.float32)
    for core in range(N_CORES):
        r = res.results[core]
        b, par = core // 2, core % 2
        y_prompt[2 * core:2 * core + 2] = r["yp"].reshape(2, 256, D)
        state_k[2 * core:2 * core + 2, 0] = r["sk"].reshape(2, 256, 16, 64)
        state_v[2 * core:2 * core + 2, 0] = r["sv"].reshape(2, 256, 16, 64)
        ys = r["ys"].reshape(16, 64, D)
        yv = y_sample[b].reshape(32, 64, D)
        if par == 0:
            yv[0:16] = ys
        else:
            yv[16:32] = ys[::-1]
    return (y_prompt, y_sample, state_k, state_v)
```

```python
import contextlib
import numpy as np
import concourse.bass as bass
import concourse.mybir as mybir
from concourse.bass_utils import run_bass_kernel_spmd

F32 = mybir.dt.float32
BF16 = mybir.dt.bfloat16
AF = mybir.ActivationFunctionType
ALU = mybir.AluOpType

D = 2048
DFF = 5632
NFG = 22
NEG = -30000.0
EPS = 1e-6
N_CORES = 8
ARENA_WORDS = 52480


class _Node:
    __slots__ = ("w", "r", "ch")

    def __init__(self):
        self.w = None
        self.r = {}
        self.ch = {}


class Sched:
    ENG = ("pe", "act", "dve", "pool", "sp")

    def __init__(self):
        self.streams = {e: [] for e in self.ENG}
        self.count = {}
        self.known = {e: {} for e in self.ENG}
        self.flag = set()
        self.clock = {}
        self.root = _Node()
        self.dma_keys = []
        self.last_real = {}

    def _walk(self, region, create):
        node = self.root
        path = [node]
        for k in region:
            nxt = node.ch.get(k)
            if nxt is None:
                if not create:
                    return path, None
                nxt = _Node()
                node.ch[k] = nxt
            node = nxt
            path.append(node)
        return path, node

    def _subtree(self, node, out, with_reads):
        stack = [node]
        while stack:
            n = stack.pop()
            if n.w is not None:
                out.add(n.w)
            if with_reads:
                for ve, p in n.r.items():
                    out.add((ve, p))
            stack.extend(n.ch.values())

    def _deps(self, reads, writes):
        deps = set()
        for reg in reads:
            path, node = self._walk(reg, False)
            for n in path:
                if n.w is not None:
                    deps.add(n.w)
            if node is not None:
                self._subtree(node, deps, False)
        for reg in writes:
            path, node = self._walk(reg, False)
            for n in path:
                if n.w is not None:
                    deps.add(n.w)
                for ve, p in n.r.items():
                    deps.add((ve, p))
            if node is not None:
                self._subtree(node, deps, True)
        return deps

    def add(self, eng, fn, reads=(), writes=(), dma_key=None, extra_deps=()):
        veng = eng if dma_key is None else "dma:" + dma_key
        if dma_key is not None and veng not in self.count:
            self.dma_keys.append(veng)
        ps_r = [r for r in reads if r[0] == "ps"]
        if ps_r:
            reads = [r for r in reads if r[0] != "ps"]
            writes = list(writes) + ps_r
        deps = self._deps(reads, writes)
        deps.update(extra_deps)
        pos = self.count.get(veng, 0) + 1
        self.count[veng] = pos
        K = self.known[eng]
        waits = []
        for (dv, dp) in sorted(deps, key=lambda d: -d[1]):
            if dv == "pe" and eng == "pe" and dma_key is None:
                continue
            if K.get(dv, 0) >= dp:
                continue
            waits.append((dv, dp))
            self.flag.add((dv, dp))
            ck = self.clock[(dv, dp)]
            for k2, v2 in ck.items():
                if K.get(k2, 0) < v2:
                    K[k2] = v2
        ck = dict(K)
        ck[veng] = pos
        self.clock[(veng, pos)] = ck
        for reg in reads:
            _, node = self._walk(reg, True)
            if node.r.get(veng, 0) < pos:
                node.r[veng] = pos
        for reg in writes:
            _, node = self._walk(reg, True)
            node.ch = {}
            node.r = {}
            node.w = (veng, pos)
        self.streams[eng].append((fn, waits, veng, pos))
        if fn is not None:
            self.last_real[veng] = pos
        return (veng, pos)

    def barrier(self, final=False):
        snap = [(ve, p) for ve, p in self.last_real.items() if p > 0]
        for e in self.ENG:
            self.add(e, None, extra_deps=[d for d in snap if not (d[0] == e and e == "pe")])
        self.root = _Node()

    def emit(self, nc, block_fn_map):
        vengs = [e for e in self.ENG if self.count.get(e, 0) > 0] + self.dma_keys
        cum = {}
        for e in self.ENG:
            c = 0
            arr = [0] * (self.count.get(e, 0) + 1)
            for p in range(1, self.count.get(e, 0) + 1):
                if (e, p) in self.flag:
                    c += 1
                arr[p] = c
            cum[e] = arr
        with contextlib.ExitStack() as es:
            sems = {}
            for i, ve in enumerate(vengs):
                sems[ve] = es.enter_context(nc.semaphore("s%d" % i))
            with nc.Block() as block:
                def run(engname, handle):
                    for (fn, waits, veng, pos) in self.streams[engname]:
                        for (dv, dp) in waits:
                            if dv.startswith("dma:"):
                                handle.wait_ge(sems[dv], 16 * dp)
                            else:
                                handle.wait_ge(sems[dv], cum[dv][dp])
                        if fn is None:
                            continue
                        inst = fn(handle)
                        if veng.startswith("dma:"):
                            inst.then_inc(sems[veng], 16)
                        elif (veng, pos) in self.flag:
                            inst.then_inc(sems[veng], 1)

                @block.tensor
                def _(h):
                    run("pe", h)

                @block.scalar
                def _(h):
                    run("act", h)

                @block.vector
                def _(h):
                    run("dve", h)

                @block.gpsimd
                def _(h):
                    run("pool", h)

                @block.sync
                def _(h):
                    run("sp", h)


class Builder:
    def __init__(self, debug=None):
        self.debug = debug
        nc = bass.Bass("TRN2", target_bir_lowering=False)
        self.nc = nc
        self.S = Sched()
        dt = nc.dram_tensor

        def din(name, shape):
            return dt(name, list(shape), F32, kind="ExternalInput").ap()

        def dout(name, shape):
            return dt(name, list(shape), F32, kind="ExternalOutput").ap()

        self.xp = din("xp", [512, D])
        self.xs = din("xs", [1280, D])
        self.ck = din("ck", [512, 1024])
        self.cv = din("cv", [512, 1024])
        self.cvec = din("cvec", [128, 32])
        self.w_ada = din("w_ada", [72, 128, 16, 256])
        self.b_ada = din("b_ada_fm", [128, 144])
        self.gains = din("gains", [128, 64])
        self.gains_mix = din("gains_mix", [128, 16])
        self.gnorm_bc = din("gnorm_bc", [128, 1024])
        self.bs_bc = din("bs_bc", [128, 1024])
        self.w_sT = din("w_sT", [128, 1024])
        self.bs_bc_s = din("bs_bc_s", [128, 1024])
        self.w_sT_s = din("w_sT_s", [128, 1024])
        self.w1g = din("w1g", [NFG, 128, 16, 256])
        self.w1u = din("w1u", [NFG, 128, 16, 256])
        self.w1d = din("w1d", [NFG, 128, 2, D])
        self.w2g = din("w2g", [NFG, 128, 16, 256])
        self.w2u = din("w2u", [NFG, 128, 16, 256])
        self.w2d = din("w2d", [NFG, 128, 2, D])
        self.w_in = din("w_in", [20, 128, 16, 256])
        self.w_out = din("w_out", [8, 128, 16, 256])
        self.biasT = din("biasT", [16, 128, 1472])
        self.ind = din("ind", [28, 1280])
        self.rnz = din("rnz", [28, 1024])
        self.identd = din("ident", [128, 128])
        self.yp = dout("yp", [512, D])
        self.ys = dout("ys", [1024, D])
        self.sk = dout("sk", [512, 1024])
        self.sv = dout("sv", [512, 1024])
        self.xscr = dt("xscr", [128, 16 * 1024], F32, kind="Internal").ap()
        if debug is not None:
            self.dbg = dout("dbg", [128, debug["n"]])

    def a_f32(self, n):
        n8 = (n + 7) // 8 * 8
        off = self.ptr
        self.ptr += n8
        assert self.ptr <= ARENA_WORDS, ("arena overflow", self.ptr)
        return self.arena[:, off:off + n]

    def a_bf(self, n):
        w = (n + 1) // 2
        w8 = (w + 7) // 8 * 8
        off = self.ptr
        self.ptr += w8
        assert self.ptr <= ARENA_WORDS, ("arena overflow", self.ptr)
        return self.abf[:, 2 * off:2 * off + n]

    def bank(self, b, n=512, dtype=F32):
        return self.ps[:, b * 512:b * 512 + n]

    def mm(self, out, lhsT, rhs, start, stop, reads, writes):
        self.S.add("pe", lambda h: h.matmul(out, lhsT=lhsT, rhs=rhs, start=start, stop=stop),
                   reads=reads, writes=writes)

    def tr(self, out, in_, ident, reads, writes):
        self.S.add("pe", lambda h: h.transpose(out, in_, ident), reads=reads, writes=writes)

    def act(self, out, in_, func, reads, writes, scale=None, bias=None):
        kw = {}
        if scale is not None:
            kw["scale"] = scale
        if bias is not None:
            kw["bias"] = bias
        self.S.add("act", lambda h: h.activation(out, in_, func, **kw), reads=reads, writes=writes)

    def tt(self, out, in0, in1, op, reads, writes, eng="dve"):
        self.S.add(eng, lambda h: h.tensor_tensor(out, in0, in1, op), reads=reads, writes=writes)

    def ts(self, out, in0, s1, s2, op0, op1, reads, writes, eng="dve"):
        if op1 is None:
            self.S.add(eng, lambda h: h.tensor_scalar(out, in0, s1, None, op0), reads=reads, writes=writes)
        else:
            self.S.add(eng, lambda h: h.tensor_scalar(out, in0, s1, s2, op0, op1), reads=reads, writes=writes)

    def stt(self, out, in0, scalar, in1, op0, op1, reads, writes):
        self.S.add("dve", lambda h: h.scalar_tensor_tensor(out, in0, scalar, in1, op0, op1),
                   reads=reads, writes=writes)

    def cp(self, eng, out, in_, reads, writes):
        if eng == "act":
            self.S.add("act", lambda h: h.copy(out, in_), reads=reads, writes=writes)
        else:
            self.S.add(eng, lambda h: h.tensor_copy(out, in_), reads=reads, writes=writes)

    def recip(self, out, in_, reads, writes):
        self.S.add("dve", lambda h: h.reciprocal(out, in_), reads=reads, writes=writes)

    def dma(self, q, out, in_, key, reads=(), writes=()):
        self.S.add(q, lambda h: h.dma_start(out=out, in_=in_), reads=reads, writes=writes, dma_key=key)

    def build(self):
        nc = self.nc
        with contextlib.ExitStack() as es:
            self.arena = es.enter_context(nc.sbuf_tensor("arena", [128, ARENA_WORDS], F32))
            self.ps = es.enter_context(nc.psum_tensor("ps", [128, 4096], F32))
            self.abf = self.arena.bitcast(BF16)
            self.ptr = 0
            self.evq = 0
            self.consts()
            base = self.ptr
            self.kT_halo = self.a_bf(8 * 256).rearrange("p (c t) -> p c t", c=8)
            self.v_halo = self.a_bf(2 * 1024).rearrange("p (s c) -> p s c", s=2)
            gbase = self.ptr
            if self.run_group("PH", gbase):
                self.ptr = gbase
                self.run_group("S", gbase)
            self.S.barrier(final=True)
            self.S.emit(nc, None)
        return nc

    def evac_engine(self):
        self.evq += 1
        return "act" if (self.evq & 1) else "dve"

    def consts(self):
        S = self.S
        self.ident = self.a_f32(128)
        self.identb = self.a_bf(128)
        self.onesb = self.a_bf(128)
        self.modT = self.a_f32(288).rearrange("p (s m) -> p s m", s=2)
        self.coef = self.a_f32(160).rearrange("p (s k c) -> p s k c", s=2, k=5)
        self.gains_t = self.a_f32(64).rearrange("p (k c) -> p k c", k=4)
        self.gmix_t = self.a_f32(16)
        self.bada_t = self.a_f32(144)
        self.cvec_t = self.a_f32(32)
        self.scT = self.a_bf(32)
        self.gnorm_t = self.a_f32(1024)
        self.bs_t = self.a_f32(1024)
        self.wsT = self.a_bf(1024).rearrange("p (g i) -> p g i", g=8)
        self.bs_s_t = self.a_f32(1024)
        self.wsT_s = self.a_bf(1024).rearrange("p (g i) -> p g i", g=8)
        self.eps_t = self.a_f32(1)
        self.dma("sp", self.ident, self.identd, "c0", writes=[("ident",)])
        self.dma("sp", self.gains_t, self.gains.rearrange("p (k c) -> p k c", k=4), "c1", writes=[("gains",)])
        self.dma("sp", self.gmix_t, self.gains_mix, "c2", writes=[("gmix",)])
        self.dma("sp", self.bada_t, self.b_ada, "c3", writes=[("bada",)])
        self.dma("sp", self.cvec_t, self.cvec, "c4", writes=[("cvec",)])
        self.dma("sp", self.gnorm_t, self.gnorm_bc, "c5", writes=[("gnorm",)])
        self.dma("sp", self.bs_t, self.bs_bc, "c6", writes=[("bs",)])
        self.dma("pool", self.wsT, self.w_sT.rearrange("p (g i) -> p g i", g=8), "c7", writes=[("wsT",)])
        self.dma("sp", self.bs_s_t, self.bs_bc_s, "c10", writes=[("bs_s",)])
        self.dma("pool", self.wsT_s, self.w_sT_s.rearrange("p (g i) -> p g i", g=8), "c11", writes=[("wsT_s",)])
        S.add("dve", lambda h: h.memset(self.eps_t, EPS), writes=[("eps",)])
        self.cp("dve", self.identb, self.ident, [("ident",)], [("identb",)])
        S.add("dve", lambda h: h.memset(self.onesb, 1.0), writes=[("onesb",)])
        self.act(self.scT, self.cvec_t, AF.Silu, [("cvec",)], [("scT",)])

    def adaln_prefetch(self):
        save = self.ptr
        self.ptr = ARENA_WORDS - 3 * 2048 - 8
        self.ada_slabs = [self.a_bf(4096).rearrange("p (c f) -> p c f", c=16) for _ in range(3)]
        self.ptr = save
        self.ada_next = 0
        self.ada_loaded = 0
        self.ada_limit = 56
        self.ada_psb = self.bank(7, 288)
        for _ in range(3):
            self._ada_load()

    def adaln_begin(self):
        self._ada_evac(0, 32)
        for s in range(2):
            m = self.modT[:, s, :].rearrange("p (m c) -> p m c", m=9)
            self.stt(self.coef[:, s, 0, :], m[:, 1, :], 1.0, self.gains_t[:, 0, :], ALU.add, ALU.mult,
                     [("modT",), ("gains",)], [("coef", s, 0)])

    def adaln_gate1(self):
        for _ in range(8):
            self.adaln_slabs(1)
            yield
        self._ada_evac(32, 48)
        for s in range(2):
            m = self.modT[:, s, :].rearrange("p (m c) -> p m c", m=9)
            self.ts(self.coef[:, s, 1, :], m[:, 2, :], 0.5, None, ALU.mult, None, [("modT",)], [("coef", s, 1)])
        yield

    def _ada_load(self):
        i = self.ada_loaded
        if i >= self.ada_limit:
            return
        self.ada_loaded += 1
        self.dma("pool", self.ada_slabs[i % 3], self.w_ada[i], "ada%d" % (i % 3),
                 writes=[("aslab", i % 3)])

    def adaln_slabs(self, n):
        for _ in range(n):
            i = self.ada_next
            if i >= self.ada_limit:
                return
            self.ada_next += 1
            for cc in range(2):
                j = 2 * i + cc
                for dc in range(16):
                    self.mm(self.ada_psb[:, 2 * j:2 * j + 2], self.ada_slabs[i % 3][:, dc, cc * 128:(cc + 1) * 128],
                            self.scT[:, 2 * dc:2 * dc + 2], dc == 0, dc == 15,
                            [("aslab", i % 3), ("scT",)], [("ps", 7)])
            self._ada_load()

    def _ada_evac(self, j0, j1):
        psv = self.ada_psb.rearrange("p (j s) -> p j s", s=2)
        for s in range(2):
            self.tt(self.modT[:, s, j0:j1], psv[:, j0:j1, s], self.bada_t[:, j0:j1], ALU.add,
                    [("ps", 7), ("bada",)], [("modT", j0)])

    def adaln_hook(self, fg):
        self.adaln_slabs(2 if (fg % 2 == 0 and fg < 20) else 1)

    def adaln_finish(self):
        self.adaln_slabs(72)
        self._ada_evac(48, 2 * self.ada_limit)
        for s in range(2):
            m = self.modT[:, s, :].rearrange("p (m c) -> p m c", m=9)
            self.stt(self.coef[:, s, 2, :], m[:, 4, :], 1.0, self.gains_t[:, 1, :], ALU.add, ALU.mult,
                     [("modT",), ("gains",)], [("coef", s, 2)])
        self.S.barrier()

    def adaln_resume(self):
        self.ada_limit = 72
        for _ in range(3):
            self._ada_load()

    def adaln_finish2(self):
        self.adaln_slabs(72)
        self._ada_evac(112, 144)
        for s in range(2):
            m = self.modT[:, s, :].rearrange("p (m c) -> p m c", m=9)
            self.stt(self.coef[:, s, 3, :], m[:, 7, :], 1.0, self.gains_t[:, 2, :], ALU.add, ALU.mult,
                     [("modT",), ("gains",)], [("coef", s, 3)])
            self.ts(self.coef[:, s, 4, :], m[:, 8, :], 0.5, None, ALU.mult, None, [("modT",)], [("coef", s, 4)])
        self.S.barrier()

    def mod(self, s, mi):
        return self.modT[:, s, mi * 16:(mi + 1) * 16]

    def normmod(self, src, dst, tiles, nch, Dn, scale_of, bias_of, srcname, dstname, tmps, psbanks):
        self.norm_stats(src, tiles, nch, Dn, srcname, tmps, psbanks)
        self.norm_apply(src, dst, tiles, nch, scale_of, bias_of, srcname, dstname, tmps)

    def norm_stats(self, src, tiles, nch, Dn, srcname, tmps, psbanks):
        sq, std, rstd, tmp = tmps
        assert len(tiles) <= 2
        for ti, (c0, n, tix) in enumerate(tiles):
            pb = psbanks[ti % len(psbanks)]
            pbank = self.bank(pb, n)
            ngr = nch // 4
            for g in range(ngr):
                sqb = sq[g % 2]
                self.act(sqb[:, :, 0:n], src[:, 4 * g:4 * g + 4, c0:c0 + n], AF.Square,
                         [(srcname, tix, c) for c in range(4 * g, 4 * g + 4)], [("sq", g % 2)])
                for r in range(4):
                    c = 4 * g + r
                    self.mm(pbank, self.onesb, sqb[:, r, 0:n], c == 0, c == nch - 1,
                            [("sq", g % 2), ("onesb",)], [("ps", pb)])
            self.act(std[ti][:, 0:n], pbank, AF.Sqrt, [("ps", pb)], [("std", ti)], scale=1.0 / Dn, bias=self.eps_t)
            self.recip(rstd[ti][:, 0:n], std[ti][:, 0:n], [("std", ti)], [("rstd", ti)])

    def norm_apply(self, src, dst, tiles, nch, scale_of, bias_of, srcname, dstname, tmps, extra_reads=()):
        sq, std, rstd, tmp = tmps
        for ti, (c0, n, tix) in enumerate(tiles):
            for c in range(nch):
                tb = tmp[c % 2]
                self.tt(tb[:, 0:n], src[:, c, c0:c0 + n], rstd[ti][:, 0:n], ALU.mult,
                        [(srcname, tix, c), ("rstd", ti)], [("ntmp", c % 2)])
                sc = scale_of(c, tix)
                bi = bias_of(c, tix) if bias_of is not None else None
                self.act(dst[:, c, c0:c0 + n], tb[:, 0:n], AF.Identity, [("ntmp", c % 2)] + list(extra_reads),
                         [(dstname, tix, c)], scale=sc, bias=bi)

    def ffn(self, G, Wg, Wu, Wd, tiles, set_of, gk, slabs, tmps, hook=None, dn_banks=(4, 5, 6, 7), first_gen=None,
            early_loads=None):
        wg, wu, wd = slabs
        sg, actb = tmps
        xT, hT = G["xT"], G["hT"]

        def load(fg, alias=()):
            b = fg % 2
            self.dma("pool", wg[b], Wg[fg], "wg%d" % b, writes=[("wg", b)] + [(a, b) for a in alias])
            self.dma("pool", wu[b], Wu[fg], "wu%d" % b, writes=[("wu", b)])
            self.dma("pool", wd[b], Wd[fg], "wd%d" % b, writes=[("wd", b)])

        if early_loads == "issue":
            load(0, alias=("stage",))
            load(1, alias=("stage",))
            return

        units = [(fg, t) for fg in range(NFG) for t in range(len(tiles))]
        state = {"it": 0, "dn": 0}

        def gateup(ui):
            fg, t = units[ui]
            c0, n, tix = tiles[t]
            b = fg % 2
            for fc in range(2):
                it = state["it"]
                state["it"] += 1
                pg, pu = it % 2, 2 + it % 2
                for dc in range(16):
                    self.mm(self.bank(pg, n), wg[b][:, dc, fc * 128:(fc + 1) * 128], hT[:, dc, c0:c0 + n],
                            dc == 0, dc == 15, [("wg", b), ("hT", tix, dc)], [("ps", pg)])
                    if dc % 4 == 3:
                        if dc == 15:
                            self.act(sg[it % 2][:, 0:n], self.bank(pg, n), AF.Silu, [("ps", pg)], [("sg", it % 2)])
                        yield
                for dc in range(16):
                    self.mm(self.bank(pu, n), wu[b][:, dc, fc * 128:(fc + 1) * 128], hT[:, dc, c0:c0 + n],
                            dc == 0, dc == 15, [("wu", b), ("hT", tix, dc)], [("ps", pu)])
                    if dc % 4 == 3:
                        if dc == 15:
                            self.tt(actb[ui % 2][:, fc, 0:n], sg[it % 2][:, 0:n], self.bank(pu, n), ALU.mult,
                                    [("sg", it % 2), ("ps", pu)], [("actb", ui % 2, fc)])
                        yield

        def down(ui):
            fg, t = units[ui]
            c0, n, tix = tiles[t]
            b = fg % 2
            s = set_of(tix)
            for dc in range(16):
                pd = dn_banks[state["dn"] % len(dn_banks)]
                state["dn"] += 1
                for fc in range(2):
                    self.mm(self.bank(pd, n), wd[b][:, fc, dc * 128:(dc + 1) * 128], actb[ui % 2][:, fc, 0:n],
                            fc == 0, fc == 1, [("wd", b), ("actb", ui % 2, fc)], [("ps", pd)])
                self.stt(xT[:, dc, c0:c0 + n], self.bank(pd, n), self.coef[:, s, gk, dc:dc + 1],
                         xT[:, dc, c0:c0 + n], ALU.mult, ALU.add,
                         [("ps", pd), ("xT", tix, dc), ("coef", s, gk)], [("xT", tix, dc)])
                if dc == 15 and t == len(tiles) - 1:
                    if fg + 2 < NFG:
                        load(fg + 2)
                    if hook is not None:
                        hook(fg)
                yield

        if early_loads != "done":
            load(0)
            load(1)
        for ui in range(len(units)):
            g = gateup(ui)
            d = down(ui - 1) if ui > 0 else None
            for ch in range(16):
                next(g)
                if ui == 0 and first_gen is not None and ch % 2 == 1:
                    next(first_gen, None)
                if d is not None and ch >= 4:
                    next(d)
                    if ch in (7, 10, 13, 15):
                        next(d)
            for _ in g:
                pass
            if ui == 0 and first_gen is not None:
                for _ in first_gen:
                    pass
            if d is not None:
                for _ in d:
                    pass
        for _ in down(len(units) - 1):
            pass

    def gelu(self, psrc, out, n, reads, writes, t1, t2, k):
        self.act(t1, psrc, AF.Square, reads, [("gt1", k)])
        self.ts(t1, t1, 0.044715, 1.0, ALU.mult, ALU.add, [("gt1", k)], [("gt1", k)])
        self.tt(t2, t1, psrc, ALU.mult, [("gt1", k)] + list(reads), [("gt2", k)])
        self.act(t2, t2, AF.Sigmoid, [("gt2", k)], [("gt2", k)], scale=1.5957691216057308)
        self.tt(out, t2, psrc, ALU.mult, [("gt2", k)] + list(reads), writes)

    def run_group(self, name, gbase):
        S = self.S
        isS = (name == "S")
        if not isS:
            n_all, n_own = 768, 512
            tiles_all = [(0, 512, 0), (512, 256, 1)]
            tiles_own = [(0, 512, 0)]
            set_of = lambda tix: 0 if tix == 0 else 1
        else:
            n_all, n_own = 1024, 1024
            tiles_all = [(0, 512, 0), (512, 512, 1)]
            tiles_own = tiles_all
            set_of = lambda tix: 1
        nsub_all, nsub_own = n_all // 128, n_own // 128
        G = {}
        self.ptr = gbase
        r0 = self.ptr
        hT = self.a_bf(16 * n_all).rearrange("p (c t) -> p c t", c=16)
        r1 = self.ptr
        xT = self.a_f32(16 * n_all).rearrange("p (c t) -> p c t", c=16)
        G["xT"], G["hT"] = xT, hT
        pbase = self.ptr

        def xrows(ts):
            if isS:
                return self.xs[ts * 128:(ts + 1) * 128, :]
            if ts < 4:
                return self.xp[ts * 128:(ts + 1) * 128, :]
            return self.xs[1024 + (ts - 4) * 128:1024 + (ts - 3) * 128, :]

        def norm_tmps():
            sq = [self.a_bf(4 * 512).rearrange("p (r t) -> p r t", r=4) for _ in range(2)]
            std = [self.a_f32(512) for _ in range(2)]
            rstd = [self.a_f32(512) for _ in range(2)]
            tmp = [self.a_f32(512) for _ in range(2)]
            return sq, std, rstd, tmp

        def ffn_bufs():
            wg = [self.a_bf(4096).rearrange("p (c f) -> p c f", c=16) for _ in range(2)]
            wu = [self.a_bf(4096).rearrange("p (c f) -> p c f", c=16) for _ in range(2)]
            wd = [self.a_bf(4096).rearrange("p (fc d) -> p fc d", fc=2) for _ in range(2)]
            sg = [self.a_f32(512) for _ in range(2)]
            actb = [self.a_bf(1024).rearrange("p (fc t) -> p fc t", fc=2) for _ in range(2)]
            return (wg, wu, wd), (sg, actb)

        if not isS:
            self.adaln_prefetch()
        nt = norm_tmps()
        fbase = self.ptr
        stage = [self.a_f32(2048) for _ in range(2)]
        for ts in range(nsub_all):
            sb = ts % 2
            tix = ts // 4
            self.dma("sp", stage[sb], xrows(ts), "xst%d" % sb, writes=[("stage", sb)])
            for q in range(4):
                pb = 4 * sb + q
                for r in range(4):
                    dc = 4 * q + r
                    self.tr(self.bank(pb)[:, r * 128:(r + 1) * 128], stage[sb][:, dc * 128:(dc + 1) * 128], self.ident,
                            [("stage", sb), ("ident",)], [("ps", pb)])
                self.cp(self.evac_engine(), xT[:, 4 * q:4 * q + 4, ts * 128:(ts + 1) * 128],
                        self.bank(pb).rearrange("p (r t) -> p r t", r=4),
                        [("ps", pb)], [("xT", tix, 4 * q + r) for r in range(4)])
        self.ptr = fbase
        slabs, ftmps = ffn_bufs()

        if not isS:
            self.adaln_slabs(16)
        self.norm_stats(xT, tiles_all, 16, D, "xT", nt, [4, 5])
        if not isS:
            self.ffn(G, self.w1g, self.w1u, self.w1d, tiles_all, set_of, 1, slabs, ftmps, early_loads="issue")
            self.adaln_begin()
            xr = [("coef",), ("modT",)]
        else:
            self.ffn(G, self.w1g, self.w1u, self.w1d, tiles_all, set_of, 1, slabs, ftmps, early_loads="issue")
            xr = []
        self.norm_apply(xT, hT, tiles_all, 16,
                        lambda c, tix: self.coef[:, set_of(tix), 0, c:c + 1],
                        lambda c, tix: self.mod(set_of(tix), 0)[:, c:c + 1],
                        "xT", "hT", nt, extra_reads=xr)
        if self.debug and self.debug.get("stop") == name + ":norm1":
            return self.dump(hT)
        if not isS:
            self.ffn(G, self.w1g, self.w1u, self.w1d, tiles_all, set_of, 1, slabs, ftmps,
                     hook=self.adaln_hook, dn_banks=(4, 5, 6), first_gen=self.adaln_gate1(), early_loads="done")
            self.adaln_finish()
        else:
            self.ffn(G, self.w1g, self.w1u, self.w1d, tiles_all, set_of, 1, slabs, ftmps, early_loads="done")
        if self.debug and self.debug.get("stop") == name + ":ffn1":
            return self.dump(xT)

        if not isS:
            save = self.ptr
            self.ptr = pbase + 5 * (8 * n_own) // 2
            ws_pre = [self.a_bf(4096).rearrange("p (c f) -> p c f", c=16) for _ in range(4)]
            self.ptr = save
            for si in range(4):
                self.dma("pool", ws_pre[si], self.w_in[si], "ws%d" % si, writes=[("ws", si)])
        self.normmod(xT, hT, tiles_all, 16, D,
                     lambda c, tix: self.coef[:, set_of(tix), 2, c:c + 1],
                     lambda c, tix: self.mod(set_of(tix), 3)[:, c:c + 1],
                     "xT", "hT", nt, [4, 5])
        if isS:
            self.dma("sp", self.xscr, self.arena_flat(xT), "spill", reads=[("xT",)], writes=[("xscr",)])
        S.barrier()
        self.ptr = r1 if isS else pbase
        kT = self.a_bf(8 * n_own).rearrange("p (c t) -> p c t", c=8)
        v = self.a_bf(nsub_own * 1024).rearrange("p (s c) -> p s c", s=nsub_own)
        qT = self.a_bf(8 * n_own).rearrange("p (c t) -> p c t", c=8)
        mbase = self.ptr
        guT = self.a_bf(8 * n_own).rearrange("p (c t) -> p c t", c=8)
        vn = self.a_bf(nsub_own * 1024).rearrange("p (s c) -> p s c", s=nsub_own)
        gbase2 = self.ptr
        ws = [self.a_bf(4096).rearrange("p (c f) -> p c f", c=16) for _ in range(4)]
        gt1 = [self.a_f32(512) for _ in range(2)]
        gt2 = [self.a_f32(512) for _ in range(2)]
        gv = [self.a_f32(1024) for _ in range(2)]
        gsq = self.a_f32(1024)
        gss = self.a_f32(8)
        gstd = self.a_f32(8)
        grs = self.a_f32(8)
        ost = [self.a_f32(256) for _ in range(2)]

        def wload(si):
            self.dma("pool", ws[si % 4], self.w_in[si], "ws%d" % (si % 4), writes=[("ws", si % 4)])

        if isS:
            for si in range(4):
                wload(si)
        else:
            assert all(ws[k].offset == ws_pre[k].offset for k in range(4))
        fmb = [0]
        tmb = [0]
        gk = [0]
        for si in range(16):
            typ, w = si // 4, si % 4
            sl = ws[si % 4]
            rs = [("ws", si % 4)]
            if typ in (0, 1, 3):
                ttiles = tiles_all if typ == 1 else tiles_own
                for (c0, n, tix) in ttiles:
                    for cc in range(2):
                        pb = fmb[0] % 4
                        fmb[0] += 1
                        for dc in range(16):
                            self.mm(self.bank(pb, n), sl[:, dc, cc * 128:(cc + 1) * 128], hT[:, dc, c0:c0 + n],
                                    dc == 0, dc == 15, rs + [("hT", tix, dc)], [("ps", pb)])
                        ch = 2 * w + cc
                        if typ == 0:
                            self.cp(self.evac_engine(), qT[:, ch, c0:c0 + n], self.bank(pb, n), [("ps", pb)], [("qT", ch, tix)])
                        elif typ == 1:
                            if (not isS) and tix == 1:
                                dstk, wr = self.kT_halo[:, ch, 0:n], [("kTh", ch)]
                            else:
                                dstk, wr = kT[:, ch, c0:c0 + n], [("kT", ch, tix)]
                            self.cp(self.evac_engine(), dstk, self.bank(pb, n), [("ps", pb)], wr)
                        else:
                            k = gk[0] % 2
                            gk[0] += 1
                            self.gelu(self.bank(pb, n), guT[:, ch, c0:c0 + n], n, [("ps", pb)], [("guT", ch, tix)],
                                      gt1[k][:, 0:n], gt2[k][:, 0:n], k)
            if typ in (1, 2):
                if typ == 1 and isS:
                    subs = []
                elif typ == 1:
                    subs = list(range(4))
                else:
                    subs = list(range(nsub_all))
                for ts in subs:
                    pb = 4 + tmb[0] % 4
                    tmb[0] += 1
                    tix = ts // 4
                    for dc in range(16):
                        self.mm(self.bank(pb, 256), hT[:, dc, ts * 128:(ts + 1) * 128], sl[:, dc, :],
                                dc == 0, dc == 15, rs + [("hT", tix, dc)], [("ps", pb)])
                    if typ == 2:
                        if (not isS) and ts >= 4:
                            dstv, wr = self.v_halo[:, ts - 4, w * 256:(w + 1) * 256], [("vh", ts - 4, w)]
                        else:
                            dstv, wr = v[:, ts, w * 256:(w + 1) * 256], [("v", ts, w)]
                        self.cp(self.evac_engine(), dstv, self.bank(pb, 256), [("ps", pb)], wr)
                    if (not isS) and ts < 4:
                        ob = tmb[0] % 2
                        self.cp(self.evac_engine(), ost[ob], self.bank(pb, 256), [("ps", pb)], [("ost", ob)])
                        dst = (self.sk if typ == 1 else self.sv)[ts * 128:(ts + 1) * 128, w * 256:(w + 1) * 256]
                        self.dma("sp", dst, ost[ob], "ost%d" % ob, reads=[("ost", ob)])
            if si + 4 < 20:
                wload(si + 4)
        gnv = self.gnorm_t
        for ts in range(nsub_own):
            tix = ts // 4
            k = ts % 2
            for half in range(2):
                pb = 4 + tmb[0] % 4
                tmb[0] += 1
                for hh in range(2):
                    si = 16 + 2 * half + hh
                    for dc in range(16):
                        self.mm(self.bank(pb)[:, hh * 256:(hh + 1) * 256], hT[:, dc, ts * 128:(ts + 1) * 128],
                                ws[si % 4][:, dc, :], dc == 0, dc == 15,
                                [("ws", si % 4), ("hT", tix, dc)], [("ps", pb)])
                self.gelu(self.bank(pb), gv[k][:, half * 512:(half + 1) * 512], 512, [("ps", pb)], [("gv", k, half)],
                          gt1[half][:, 0:512], gt2[half][:, 0:512], half)
            self.act(gsq, gv[k], AF.Square, [("gv", k)], [("gsq",)])
            S.add("dve", lambda h, o=gss, i=gsq: h.tensor_reduce(o, i.rearrange("p (g c) -> p g c", g=8),
                                                                  mybir.AxisListType.X, ALU.add),
                  reads=[("gsq",)], writes=[("gss",)])
            self.act(gstd, gss, AF.Sqrt, [("gss",)], [("gstd",)], scale=1.0 / 128, bias=self.eps_t)
            self.recip(grs, gstd, [("gstd",)], [("grs",)])
            g3 = gv[k].rearrange("p (g c) -> p g c", g=8)
            self.tt(g3, g3, grs.unsqueeze(2).broadcast_to([128, 8, 128]), ALU.mult,
                    [("gv", k), ("grs",)], [("gv", k)])
            self.tt(vn[:, ts, :], gv[k], gnv, ALU.mult, [("gv", k), ("gnorm",)], [("vn", ts)])
        if self.debug and self.debug.get("stop") == name + ":mixin":
            which = self.debug.get("which", "qT")
            return self.dump({"qT": qT, "kT": kT, "v": v, "guT": guT, "vn": vn}[which])

        S.barrier()
        self.ptr = r0
        mix = self.a_bf(16 * n_own).rearrange("p (c t) -> p c t", c=16)
        assert self.ptr <= r1
        self.ptr = gbase2
        gtmp = [self.a_f32(512) for _ in range(2)]
        if isS:
            self.attn_sample_prep()
        else:
            self.adaln_resume()
        bs_use = self.bs_s_t if isS else self.bs_t
        ws_use = self.wsT_s if isS else self.wsT
        wsname = "wsT_s" if isS else "wsT"
        bsname = "bs_s" if isS else "bs"
        gi = 0
        for ts in range(nsub_own):
            tix = ts // 4
            for half in range(2):
                pb = gi % 4
                k = gi % 2
                gi += 1
                for q in range(4):
                    g = half * 4 + q
                    self.mm(self.bank(pb)[:, q * 128:(q + 1) * 128], vn[:, ts, g * 128:(g + 1) * 128], ws_use[:, g, :],
                            True, True, [("vn", ts), (wsname,)], [("ps", pb)])
                self.tt(gtmp[k], self.bank(pb), bs_use[:, half * 512:(half + 1) * 512], ALU.add,
                        [("ps", pb), (bsname,)], [("gtmp", k)])
                self.tt(mix[:, 8 + 4 * half:12 + 4 * half, ts * 128:(ts + 1) * 128],
                        gtmp[k].rearrange("p (g i) -> p g i", g=4),
                        guT[:, 4 * half:4 * half + 4, ts * 128:(ts + 1) * 128], ALU.mult,
                        [("gtmp", k)] + [("guT", 4 * half + q, tix) for q in range(4)],
                        [("mix", 8 + 4 * half + q, tix) for q in range(4)])
        S.barrier()
        self.ptr = mbase

        if not isS:
            self.attn_prompt(qT, kT, v, mix)
        else:
            self.attn_sample(qT, kT, v, mix)
        if self.debug and self.debug.get("stop") == name + ":attn":
            return self.dump(mix)
        S.barrier()
        self.ptr = pbase

        if isS:
            self.dma("sp", self.arena_flat(xT), self.xscr, "reload", reads=[("xscr",)], writes=[("xT",)])
        nt = norm_tmps()
        wo = [self.a_bf(4096).rearrange("p (c f) -> p c f", c=16) for _ in range(4)]

        def oload(i):
            self.dma("pool", wo[i % 4], self.w_out[i], "wo%d" % (i % 4), writes=[("wo", i % 4)])

        for i in range(4):
            oload(i)
        sset = 1 if isS else 0
        for half in range(2):
            self.normmod(mix[:, 8 * half:8 * half + 8, :], mix[:, 8 * half:8 * half + 8, :], tiles_own, 8, 1024,
                         lambda c, tix, half=half: self.gmix_t[:, 8 * half + c:8 * half + c + 1], None,
                         "mix%d" % half, "mix%d" % half, nt, [5, 6])
        ob = 0
        for i in range(8):
            for (c0, n, tix) in tiles_own:
                for cc in range(2):
                    pb = ob % 4
                    ob += 1
                    dcx = 2 * i + cc
                    for mc in range(16):
                        self.mm(self.bank(pb, n), wo[i % 4][:, mc, cc * 128:(cc + 1) * 128], mix[:, mc, c0:c0 + n],
                                mc == 0, mc == 15, [("wo", i % 4), ("mix%d" % (mc // 8), tix, mc % 8)], [("ps", pb)])
                    self.stt(xT[:, dcx, c0:c0 + n], self.bank(pb, n), self.mod(sset, 5)[:, dcx:dcx + 1],
                             xT[:, dcx, c0:c0 + n], ALU.mult, ALU.add,
                             [("ps", pb), ("xT", tix, dcx), ("modT",)], [("xT", tix, dcx)])
            if i + 4 < 8:
                oload(i + 4)
            if not isS:
                self.adaln_slabs(1)
        if self.debug and self.debug.get("stop") == name + ":mixout":
            return self.dump(xT)
        S.barrier()
        self.ptr = pbase

        nt = norm_tmps()
        slabs, ftmps = ffn_bufs()
        if not isS:
            self.adaln_finish2()
        self.normmod(xT, hT, tiles_own, 16, D,
                     lambda c, tix: self.coef[:, sset, 3, c:c + 1],
                     lambda c, tix: self.mod(sset, 6)[:, c:c + 1],
                     "xT", "hT", nt, [4, 5])
        self.ffn(G, self.w2g, self.w2u, self.w2d, tiles_own, lambda tix: sset, 4, slabs, ftmps)
        wg_ = slabs[0]
        ostage = [wg_[sb].rearrange("p c f -> p (c f)").bitcast(F32) for sb in range(2)]
        ydst = self.ys if isS else self.yp
        self.norm_stats(xT, tiles_own, 16, D, "xT", nt, [4, 5])
        for ti, (c0, n, tix) in enumerate(tiles_own):
            self.norm_apply(xT, xT, [tiles_own[ti]], 16,
                            lambda c, tix_: self.gains_t[:, 3, c:c + 1], None, "xT", "xT",
                            (nt[0], nt[1][ti:ti + 1], nt[2][ti:ti + 1], nt[3]))
            for ts in range(c0 // 128, (c0 + n) // 128):
                sb = ts % 2
                for q in range(4):
                    pb = 4 * sb + q
                    for r in range(4):
                        dc = 4 * q + r
                        self.tr(self.bank(pb)[:, r * 128:(r + 1) * 128], xT[:, dc, ts * 128:(ts + 1) * 128], self.ident,
                                [("xT", tix, dc), ("ident",)], [("ps", pb)])
                    self.cp(self.evac_engine(), ostage[sb][:, q * 512:(q + 1) * 512], self.bank(pb),
                            [("ps", pb)], [("ostage", sb, q), ("wg", sb)])
                self.dma("sp", ydst[ts * 128:(ts + 1) * 128, :], ostage[sb], "yst%d" % sb,
                         reads=[("ostage", sb), ("wg", sb)])
        S.barrier()
        return True

    def arena_flat(self, view3):
        return view3.rearrange("p c t -> p (c t)")

    def attn_prompt(self, qT, kT, v, mix):
        E = [self.a_bf(512) for _ in range(4)]
        rden = [self.a_f32(256) for _ in range(2)]
        jobs = [(s, w, hh) for s in range(2) for w in range(8) for hh in range(2)]

        def stage_a(it):
            s, w, hh = jobs[it]
            off = 64 * hh
            pb = it % 4
            for kc in range(2):
                tsb = 2 * s + kc
                self.mm(self.bank(pb)[:, kc * 256:(kc + 1) * 256], kT[off:off + 64, w, tsb * 128:(tsb + 1) * 128],
                        qT[off:off + 64, w, s * 256:(s + 1) * 256], True, True,
                        [("kT", w, 0), ("qT", w, 0)], [("ps", pb)])
            self.act(E[pb], self.bank(pb), AF.Exp, [("ps", pb)], [("E", pb)], scale=0.125)

        def stage_b(it):
            s, w, hh = jobs[it]
            po = 4 + (it % 2)
            eb = E[it % 4]
            for kc in range(2):
                tsb = 2 * s + kc
                self.mm(self.bank(po)[:, 0:256], v[:, tsb, w * 128:(w + 1) * 128], eb[:, kc * 256:(kc + 1) * 256],
                        kc == 0, kc == 1, [("v", tsb), ("E", it % 4)], [("ps", po)])
            for kc in range(2):
                self.mm(self.bank(po)[:, 256:512], self.onesb, eb[:, kc * 256:(kc + 1) * 256], kc == 0, kc == 1,
                        [("onesb",), ("E", it % 4)], [("ps", po)])
            if it % 3 == 2 and it >= 5:
                self.adaln_slabs(1)

        def stage_c(it):
            s, w, hh = jobs[it]
            off = 64 * hh
            po = 4 + (it % 2)
            rd = rden[it % 2]
            self.act(rd[off:off + 64], self.bank(po)[off:off + 64, 256:512], AF.Ln, [("ps", po)], [("rden", it % 2)])
            self.act(rd[off:off + 64], rd[off:off + 64], AF.Exp, [("rden", it % 2)], [("rden", it % 2)], scale=-1.0)
            self.tt(mix[off:off + 64, w, s * 256:(s + 1) * 256], self.bank(po)[off:off + 64, 0:256],
                    rd[off:off + 64], ALU.mult, [("ps", po), ("rden", it % 2)], [("mix", w, 0, hh, s)])

        n = len(jobs)
        for it in range(n + 3):
            if it < n:
                stage_a(it)
            if 0 <= it - 2 < n:
                stage_b(it - 2)
            if 0 <= it - 3 < n:
                stage_c(it - 3)

    def attn_sample_prep(self):
        S = self.S
        ckT = self.a_bf(8 * 512).rearrange("p (c t) -> p c t", c=8)
        cvb = self.a_bf(4 * 1024).rearrange("p (s c) -> p s c", s=4)
        cst = [self.a_f32(1024) for _ in range(2)]
        kTz = [self.a_bf(1280) for _ in range(2)]
        qTz = [self.a_bf(1024) for _ in range(2)]
        ckTz = [self.a_bf(512) for _ in range(2)]
        for hh in range(2):
            for buf, nm in ((kTz, "kTz"), (qTz, "qTz"), (ckTz, "ckTz")):
                S.add("dve", lambda h, b=buf[hh]: h.memset(b, 0.0), writes=[(nm, hh)])
            o2 = 64 * (1 - hh)
            self.dma("pool", kTz[hh][o2:o2 + 28], self.ind, "ind%d" % hh, writes=[("kTz", hh)])
            self.dma("pool", qTz[hh][o2:o2 + 28], self.rnz, "rnz%d" % hh, writes=[("qTz", hh)])
        self.dma("pool", cvb, self.cv.rearrange("(s p) c -> p s c", p=128), "cvb", writes=[("cvb",)])
        for ts in range(4):
            sb = ts % 2
            self.dma("sp", cst[sb], self.ck[ts * 128:(ts + 1) * 128, :], "cst%d" % sb, writes=[("cst", sb)])
            for q in range(2):
                pb = 4 + 2 * sb + q
                for r in range(4):
                    c = 4 * q + r
                    self.tr(self.bank(pb)[:, r * 128:(r + 1) * 128], cst[sb][:, c * 128:(c + 1) * 128], self.ident,
                            [("cst", sb), ("ident",)], [("ps", pb)])
                self.cp(self.evac_engine(), ckT[:, 4 * q:4 * q + 4, ts * 128:(ts + 1) * 128],
                        self.bank(pb).rearrange("p (r t) -> p r t", r=4), [("ps", pb)],
                        [("ckT", 4 * q + r, ts) for r in range(4)])
        self.attn_bufs = (ckT, cvb, kTz, qTz, ckTz)

    def attn_sample(self, qT, kT, v, mix):
        S = self.S
        ckT, cvb, kTz, qTz, ckTz = self.attn_bufs
        ebst = self.a_f32(1472)
        eb = [self.a_bf(1472) for _ in range(2)]
        E = [self.a_bf(512) for _ in range(6)]
        PT = [self.a_bf(512) for _ in range(4)]
        rden = [self.a_f32(512) for _ in range(2)]
        chunks = {0: [0, 2, 4, 6, 8, 10], 1: [4, 6, 8, 10, 12, 14, 16, 18]}
        posof = {}
        p = 0
        for t in (0, 1):
            for l0 in chunks[t]:
                posof[(t, l0)] = p
                p += 1
        def head_copies(hd):
            w_, hh_ = hd // 2, hd % 2
            o_ = 64 * hh_
            self.cp("dve", kTz[hh_][o_:o_ + 64, 0:1024], kT[o_:o_ + 64, w_, :], [("kT", w_)], [("kTz", hh_)])
            self.cp("dve", kTz[hh_][o_:o_ + 64, 1024:1280], self.kT_halo[o_:o_ + 64, w_, :], [("kTh", w_)], [("kTz", hh_)])
            self.cp("dve", qTz[hh_][o_:o_ + 64, :], qT[o_:o_ + 64, w_, :], [("qT", w_)], [("qTz", hh_)])
            self.cp("dve", ckTz[hh_][o_:o_ + 64, :], ckT[o_:o_ + 64, w_, :], [("ckT", w_)], [("ckTz", hh_)])

        def _valid(par, l, j):
            gj = j if par == 0 else 31 - j
            gl = l if par == 0 else 31 - l
            rs = min(max(gj - 4, 0), 24)
            return rs <= gl <= rs + 7

        jrange = {}
        for t_ in (0, 1):
            for l0_ in chunks[t_]:
                js = [jj for jj in range(8) if any(_valid(p_, l0_ + a_, 8 * t_ + jj) for p_ in (0, 1) for a_ in (0, 1))]
                jrange[(t_, l0_)] = (min(js), max(js) + 1)

        def head_eb(hd):
            self.dma("sp", ebst, self.biasT[hd], "ebst", writes=[("ebst",)])
            self.act(eb[hd % 2], ebst, AF.Exp, [("ebst",)], [("eb", hd % 2)])

        def qk(job, cidx, ci):
            w, hh, t, h, off, q0, po, pd, nch, it = job
            ebb = eb[h % 2]
            pb = ci % 4
            ptb = PT[ci % 4]
            if cidx >= 4:
                l0 = chunks[t][cidx - 4]
                ja, jb = jrange[(t, l0)]
                ca, cb = ja * 64, jb * 64
                if l0 < 16:
                    vsrc = v[:, l0 // 2, w * 128:(w + 1) * 128]
                    vreg = ("v", l0 // 2)
                else:
                    vsrc = self.v_halo[:, (l0 - 16) // 2, w * 128:(w + 1) * 128]
                    vreg = ("vh", (l0 - 16) // 2)
                self.mm(self.bank(pb)[:, ca:cb], kTz[hh][:, l0 * 64:l0 * 64 + 128], qTz[hh][:, q0 + ca:q0 + cb], True, True,
                        [("kTz", hh), ("qTz", hh)], [("ps", pb)])
                ebuf = E[ci % 6]
                self.act(ebuf[:, ca:cb], self.bank(pb)[:, ca:cb], AF.Exp, [("ps", pb)], [("E", ci % 6)], scale=0.125)
                ei0 = 8 * t - l0 + 11
                self.tt(ptb[:, ca:cb], ebuf[:, ca:cb], ebb[:, (ei0 + ja) * 64:(ei0 + jb) * 64], ALU.mult,
                        [("E", ci % 6), ("eb", h % 2)], [("PT", ci % 4)])
            else:
                cc = cidx
                ca, cb = 0, 512
                self.mm(self.bank(pb), ckTz[hh][:, cc * 128:(cc + 1) * 128], qTz[hh][:, q0:q0 + 512], True, True,
                        [("ckTz", hh), ("qTz", hh)], [("ps", pb)])
                self.act(ptb, self.bank(pb), AF.Exp, [("ps", pb)], [("PT", ci % 4)], scale=0.125)
                vsrc = cvb[:, cc, w * 128:(w + 1) * 128]
                vreg = ("cvb",)
            return (vsrc, vreg, ptb, ci % 4, ca, cb)

        def pv(job, cidx, st):
            w, hh, t, h, off, q0, po, pd, nch, it = job
            vsrc, vreg, ptb, pti, ca, cb = st
            self.mm(self.bank(po)[:, ca:cb], vsrc, ptb[:, ca:cb], cidx == 0, cidx == nch - 1,
                    [vreg, ("PT", pti)], [("ps", po)])
            self.mm(self.bank(pd)[:, ca:cb], self.onesb, ptb[:, ca:cb], cidx == 0, cidx == nch - 1,
                    [("onesb",), ("PT", pti)], [("ps", pd)])

        def finalize(job):
            w, hh, t, h, off, q0, po, pd, nch, it = job
            rd = rden[it % 2]
            self.act(rd[off:off + 64], self.bank(pd)[off:off + 64], AF.Ln, [("ps", pd)], [("rden", it % 2)])
            self.act(rd[off:off + 64], rd[off:off + 64], AF.Exp, [("rden", it % 2)], [("rden", it % 2)], scale=-1.0)
            self.tt(mix[off:off + 64, w, q0:q0 + 512], self.bank(po)[off:off + 64], rd[off:off + 64], ALU.mult,
                    [("ps", po), ("rden", it % 2)], [("mix", w, t, hh)])

        head_copies(0)
        head_eb(0)
        steps = []
        it = 0
        for h in range(16):
            w, hh = h // 2, h % 2
            for t in range(2):
                nch = len(chunks[t]) + 4
                job = (w, hh, t, h, 64 * hh, t * 512, 4 + (it % 2), 6 + (it % 2), nch, it)
                it += 1
                for cidx in range(nch):
                    steps.append((job, cidx))
        staged = {}

        def retire(k):
            jb, cb = steps[k]
            pv(jb, cb, staged.pop(k))
            if cb == jb[8] - 1:
                finalize(jb)

        for k, (job, cidx) in enumerate(steps):
            if cidx == 0 and job[2] == 1 and job[3] + 1 < 16:
                head_copies(job[3] + 1)
                head_eb(job[3] + 1)
            staged[k] = qk(job, cidx, k)
            if k >= 2:
                retire(k - 2)
        retire(len(steps) - 2)
        retire(len(steps) - 1)

    def dump(self, view):
        S = self.S
        S.barrier()
        n = self.debug["n"]
        shp = view.shape
        if len(shp) == 3:
            flat = view.rearrange("p c t -> p (c t)")
        else:
            flat = view
        if flat.dtype != F32:
            self.ptr = ARENA_WORDS - ((n + 7) // 8 * 8) - 8
            tmp = self.a_f32(n)
            self.cp("dve", tmp, flat[:, 0:n], [], [("dbgtmp",)])
            flat = tmp
        self.dma("sp", self.dbg, flat[:, 0:n], "dbg", reads=[("dbgtmp",)])
        return False


def _fm(vec, nch):
    return np.ascontiguousarray(np.asarray(vec, np.float32).reshape(nch, 128).T)


def _bias_table(rpb_l, par):
    kc = np.arange(64)[:, None]
    qc = np.arange(64)[None, :]
    dcidx = np.clip(kc - qc + 15, 0, 30)
    cs = np.clip(qc - 8, 0, 48)
    colvalid = (kc >= cs) & (kc < cs + 16)
    T = np.zeros((16, 64, 23, 64), np.float32)
    for ei in range(23):
        e = ei - 11
        if abs(e) <= 7:
            dr = -e if par == 0 else e
            vals = rpb_l[:, dr + 7][:, dcidx]
            T[:, :, ei, :] = np.where(colvalid[None], vals, np.float32(NEG))
    T2 = np.zeros_like(T)
    T2[:, :, 1:, :] = T[:, :, :-1, :]
    T = np.concatenate([T, T2], axis=1)
    return np.ascontiguousarray(T.reshape(16, 128, 23 * 64))


def _masks(par):
    chunks = {0: [0, 2, 4, 6, 8, 10], 1: [4, 6, 8, 10, 12, 14, 16, 18]}
    ind = np.zeros((28, 1280), np.float32)
    rnz = np.zeros((28, 1024), np.float32)
    pos = 0
    for t in (0, 1):
        for l0 in chunks[t]:
            for a in range(2):
                l = l0 + a
                ind[2 * pos + a, l * 64:(l + 1) * 64] = 1.0
                for jj in range(8):
                    j = 8 * t + jj
                    gj = j if par == 0 else 31 - j
                    gl = l if par == 0 else 31 - l
                    rs = min(max(gj - 4, 0), 24)
                    valid = rs <= gl <= rs + 7
                    rnz[2 * pos + a, j * 64:(j + 1) * 64] = 0.0 if valid else NEG
            pos += 1
    return ind, rnz


_NC_CACHE = {}


def _get_nc(debug=None):
    key = None if debug is None else tuple(sorted(debug.items()))
    if key not in _NC_CACHE:
        _NC_CACHE[key] = Builder(debug).build()
    return _NC_CACHE[key]


def make_in_maps(x_prompt, x_sample, cache_k, cache_v, c, c_ctx, w_ada, b_ada,
                 ffn1_norm, ffn1_w_gate, ffn1_w_up, ffn1_w_down,
                 mix_norm, w_in, rpb, gmlp_norm, w_s, b_s, out_norm_a, out_norm_b, w_out,
                 ffn2_norm, ffn2_w_gate, ffn2_w_up, ffn2_w_down, final_norm):
    f = lambda a: np.ascontiguousarray(np.asarray(a, np.float32))

    def kslab(w):
        w = np.asarray(w, np.float32)
        n = w.shape[1] // 256
        return np.ascontiguousarray(w.reshape(16, 128, n, 256).transpose(2, 1, 0, 3))

    def dslab(w):
        w = np.asarray(w, np.float32)
        return np.ascontiguousarray(w.reshape(NFG, 2, 128, D).transpose(0, 2, 1, 3))
    shared = {
        "w_ada": kslab(w_ada[0]),
        "b_ada_fm": _fm(b_ada[0], 144),
        "gains": np.ascontiguousarray(np.concatenate(
            [_fm(ffn1_norm[0], 16), _fm(mix_norm[0], 16), _fm(ffn2_norm[0], 16), _fm(final_norm, 16)], axis=1)),
        "gains_mix": np.ascontiguousarray(np.concatenate([_fm(out_norm_a[0], 8), _fm(out_norm_b[0], 8)], axis=1)),
        "gnorm_bc": np.ascontiguousarray(np.broadcast_to(np.asarray(gmlp_norm[0], np.float32)[None, :], (128, 1024))),
        "bs_bc": np.ascontiguousarray(np.broadcast_to(np.asarray(b_s[0], np.float32).reshape(1, 1024), (128, 1024))),
        "w_sT": np.ascontiguousarray(np.asarray(w_s[0], np.float32).transpose(2, 0, 1).reshape(128, 1024)),
        "w1g": kslab(ffn1_w_gate[0]), "w1u": kslab(ffn1_w_up[0]), "w1d": dslab(ffn1_w_down[0]),
        "w2g": kslab(ffn2_w_gate[0]), "w2u": kslab(ffn2_w_up[0]), "w2d": dslab(ffn2_w_down[0]),
        "w_in": kslab(w_in[0]), "w_out": kslab(w_out[0]),
        "ident": np.eye(128, dtype=np.float32),
    }
    rpb_l = np.asarray(rpb[0], np.float32)
    par_tabs = {}
    for par in (0, 1):
        ind, rnz = _masks(par)
        par_tabs[par] = (_bias_table(rpb_l, par), ind, rnz)
    in_maps = []
    for core in range(N_CORES):
        b, par = core // 2, core % 2
        xs_full = np.asarray(x_sample[b], np.float32).reshape(32, 64, D)
        if par == 1:
            xs_full = xs_full[::-1]
        xs = np.ascontiguousarray(xs_full[0:20].reshape(1280, D))
        cv2 = np.stack([_fm(c_ctx, 16), _fm(c[b], 16)], axis=2).reshape(128, 32)
        ws_l = np.asarray(w_s[0], np.float32)
        bs_l = np.asarray(b_s[0], np.float32)
        if par == 1:
            perm = np.concatenate([np.arange(64, 128), np.arange(0, 64)])
            ws_l = ws_l[:, perm][:, :, perm]
            bs_l = bs_l[:, perm]
        m = dict(shared)
        m.update({
            "w_sT_s": np.ascontiguousarray(ws_l.transpose(2, 0, 1).reshape(128, 1024)),
            "bs_bc_s": np.ascontiguousarray(np.broadcast_to(bs_l.reshape(1, 1024), (128, 1024))),
            "xp": np.ascontiguousarray(np.asarray(x_prompt[2 * core:2 * core + 2], np.float32).reshape(512, D)),
            "xs": xs,
            "ck": np.ascontiguousarray(np.asarray(cache_k[b, 0], np.float32).reshape(512, 1024)),
            "cv": np.ascontiguousarray(np.asarray(cache_v[b, 0], np.float32).reshape(512, 1024)),
            "cvec": np.ascontiguousarray(cv2),
            "biasT": par_tabs[par][0], "ind": par_tabs[par][1], "rnz": par_tabs[par][2],
        })
        in_maps.append(m)
    return in_maps


def kernel(**inputs):
    nc = _get_nc()
    in_maps = make_in_maps(**inputs)
    res = run_bass_kernel_spmd(nc, in_maps, core_ids=list(range(N_CORES)))
    y_prompt = np.zeros((16, 256, D), np.float32)
    y_sample = np.zeros((4, 2048, D), np.float32)
    state_k = np.zeros((16, 1, 256, 16, 64), np.float32)
    state_v = np.zeros((16, 1, 256, 16, 64), np.float32)
    for core in range(N_CORES):
        r = res.results[core]
        b, par = core // 2, core % 2
        y_prompt[2 * core:2 * core + 2] = r["yp"].reshape(2, 256, D)
        state_k[2 * core:2 * core + 2, 0] = r["sk"].reshape(2, 256, 16, 64)
        state_v[2 * core:2 * core + 2, 0] = r["sv"].reshape(2, 256, 16, 64)
        ys = r["ys"].reshape(16, 64, D)
        yv = y_sample[b].reshape(32, 64, D)
        if par == 0:
            yv[0:16] = ys
        else:
            yv[16:32] = ys[::-1]
    return (y_prompt, y_sample, state_k, state_v)
```
